# Optimizing a Trainium2 kernel written in Bass

```python
import math
import jax, jax.numpy as jnp
from jax import lax
import numpy as np

D_MODEL = 4096
BATCH = 4
SEQ = 4096
DEPTH = 4
DEC_BATCH = 8
DEC_SEQ = 2048
PAST_LEN = 128

HEAD_DIM = 128
ATTN_GROUPS = ((128, 1), (512, 4), (2048, 16))
HEADS_PER_GROUP = 4
ATT_HEADS = HEADS_PER_GROUP * len(ATTN_GROUPS)
ATT_W = ATT_HEADS * HEAD_DIM
ATT_OUT = HEADS_PER_GROUP * HEAD_DIM
ROPE_THETA = 10000.0
D_INNER = D_MODEL // 2
SSM_HEADDIM = 64
SSM_HEADS = D_INNER // SSM_HEADDIM
SSM_GROUPS = 4
D_STATE = 128
CONV_WIDTH = 3
CONV_PAD = CONV_WIDTH // 2
CONV_DIM = D_INNER + 2 * SSM_GROUPS * D_STATE
CHUNK = 128
D_FF = D_MODEL // 2
N_IN = 3 * ATT_W + D_INNER + CONV_DIM + 2 * SSM_HEADS + 2 * D_MODEL
RMS_EPS = 1e-6
NEG_INF = -1e30

kernel_name = 'hybrid_dilated_attn_ssd_macaron_encoder'


def rms_norm(x, w):
    xf = x.astype(jnp.float32)
    y = xf * lax.rsqrt(jnp.mean(xf * xf, axis=-1, keepdims=True) + RMS_EPS)
    return (y * w.astype(jnp.float32)).astype(x.dtype)


def swiglu(h, w_gate, w_up, w_down):
    return (jax.nn.silu(h @ w_gate) * (h @ w_up)) @ w_down


def rope(x, positions):
    inv_freq = ROPE_THETA ** (-jnp.arange(0, HEAD_DIM, 2, dtype=jnp.float32) / HEAD_DIM)
    ang = positions[:, None] * inv_freq[None, :]
    cos = jnp.cos(ang)[None, :, None, :]
    sin = jnp.sin(ang)[None, :, None, :]
    xf = x.astype(jnp.float32)
    x1, x2 = jnp.split(xf, 2, axis=-1)
    return jnp.concatenate([x1 * cos - x2 * sin, x2 * cos + x1 * sin], axis=-1)


def dilated_window_attention(q, k, v, window, dil):
    b, s, h, hd = q.shape
    R = window // (2 * dil)
    n = s // dil
    nb = -(-n // R)
    n_pad = nb * R
    bb = b * dil

    def to_sub(t):
        return t.reshape(b, n, dil, h, hd).transpose(0, 2, 1, 3, 4).reshape(bb, n, h, hd)

    qs, ks, vs = to_sub(q), to_sub(k), to_sub(v)
    qb = jnp.pad(qs, ((0, 0), (0, n_pad - n), (0, 0), (0, 0))).reshape(bb, nb, R, h, hd)
    kv_pad = ((0, 0), (R, n_pad - n + R), (0, 0), (0, 0))
    kp = jnp.pad(ks, kv_pad).reshape(bb, nb + 2, R, h, hd)
    vp = jnp.pad(vs, kv_pad).reshape(bb, nb + 2, R, h, hd)
    kb = jnp.concatenate([kp[:, :-2], kp[:, 1:-1], kp[:, 2:]], axis=2)
    vb = jnp.concatenate([vp[:, :-2], vp[:, 1:-1], vp[:, 2:]], axis=2)

    blk = jnp.arange(nb)[:, None] * R
    qpos = blk + jnp.arange(R)[None, :]
    kpos = blk - R + jnp.arange(3 * R)[None, :]
    rel = kpos[:, None, :] - qpos[:, :, None]
    valid = (jnp.abs(rel) <= R) & (kpos >= 0)[:, None, :] & (kpos < n)[:, None, :]

    scores = jnp.einsum('bnqhd,bnkhd->bnhqk', qb, kb).astype(jnp.float32)
    scores = jnp.where(valid[None, :, None, :, :], scores, NEG_INF)
    m = jnp.max(scores, axis=-1, keepdims=True)
    p = jnp.exp(scores - m)
    den = jnp.sum(p, axis=-1)
    o = jnp.einsum('bnhqk,bnkhd->bnqhd', p, vb.astype(jnp.float32))
    o = o / jnp.transpose(den, (0, 1, 3, 2))[..., None]
    lse = jnp.transpose(m[..., 0] + jnp.log(den), (0, 1, 3, 2))

    o = o.reshape(bb, n_pad, h, hd)[:, :n].reshape(b, dil, n, h, hd)
    o = o.transpose(0, 2, 1, 3, 4).reshape(b, s, h, hd)
    lse = lse.reshape(bb, n_pad, h)[:, :n].reshape(b, dil, n, h)
    lse = lse.transpose(0, 2, 1, 3).reshape(b, s, h)
    return o, lse


def attention_branch(q, k, v, q_norm, k_norm):
    b, s = q.shape[0], q.shape[1]
    positions = jnp.arange(s, dtype=jnp.float32)
    qn = rope(rms_norm(q, q_norm), positions) * (HEAD_DIM ** -0.5)
    kn = rope(rms_norm(k, k_norm), positions)
    vf = v.astype(jnp.float32)
    outs, lses = [], []
    for g, (window, dil) in enumerate(ATTN_GROUPS):
        sl = slice(g * HEADS_PER_GROUP, (g + 1) * HEADS_PER_GROUP)
        o, lse = dilated_window_attention(qn[:, :, sl], kn[:, :, sl], vf[:, :, sl], window, dil)
        outs.append(o)
        lses.append(lse)
    wts = jax.nn.softmax(jnp.stack(lses, axis=0), axis=0)
    o = jnp.sum(wts[..., None] * jnp.stack(outs, axis=0), axis=0)
    return o.reshape(b, s, ATT_OUT).astype(q.dtype)


def ssd_scan(x, dt, A, Bm, Cm):
    b, s = x.shape[0], x.shape[1]
    G, J, P, N = SSM_GROUPS, SSM_HEADS // SSM_GROUPS, SSM_HEADDIM, D_STATE
    nc = s // CHUNK
    f32 = jnp.float32
    xs = jnp.moveaxis(x.astype(f32).reshape(b, nc, CHUNK, G, J, P), 1, 0)
    dts = jnp.moveaxis(dt.reshape(b, nc, CHUNK, G, J), 1, 0)
    Bs = jnp.moveaxis(Bm.astype(f32).reshape(b, nc, CHUNK, G, N), 1, 0)
    Cs = jnp.moveaxis(Cm.astype(f32).reshape(b, nc, CHUNK, G, N), 1, 0)
    Ag = A.reshape(G, J)
    tril = jnp.tril(jnp.ones((CHUNK, CHUNK), dtype=bool))[None, :, :, None, None]

    def step(state, inp):
        xc, dtc, Bc, Cc = inp
        acum = jnp.cumsum(dtc * Ag, axis=1)
        diff = acum[:, :, None] - acum[:, None, :]
        decay = jnp.exp(jnp.where(tril, diff, -jnp.inf))
        cb = jnp.einsum('blgn,bsgn->blsg', Cc, Bc)
        scores = cb[..., None] * decay * dtc[:, None]
        y = jnp.einsum('blsgj,bsgjp->blgjp', scores, xc)
        y = y + jnp.einsum('blgn,bgjpn->blgjp', Cc, state) * jnp.exp(acum)[..., None]
        w_end = jnp.exp(acum[:, -1:] - acum) * dtc
        state = state * jnp.exp(acum[:, -1])[..., None, None] + jnp.einsum(
            'blgn,blgj,blgjp->bgjpn', Bc, w_end, xc)
        return state, y

    state0 = jnp.zeros((b, G, J, P, N), f32)
    _, ys = lax.scan(step, state0, (xs, dts, Bs, Cs))
    return jnp.moveaxis(ys, 0, 1).reshape(b, s, SSM_HEADS, P)


def ssd_branch(z, xbc, dt_raw, conv_w, conv_b, dt_bias, a_log, d_skip, ssm_norm):
    b, s = z.shape[0], z.shape[1]
    xbc = lax.conv_general_dilated(
        xbc, conv_w[:, None, :].astype(xbc.dtype), window_strides=(1,),
        padding=((CONV_PAD, CONV_PAD),), dimension_numbers=('NWC', 'WIO', 'NWC'),
        feature_group_count=CONV_DIM)
    xbc = jax.nn.silu(xbc + conv_b)
    xs, Bm, Cm = jnp.split(xbc, [D_INNER, D_INNER + SSM_GROUPS * D_STATE], axis=-1)
    xs = xs.reshape(b, s, SSM_HEADS, SSM_HEADDIM)
    Bm = Bm.reshape(b, s, SSM_GROUPS, D_STATE)
    Cm = Cm.reshape(b, s, SSM_GROUPS, D_STATE)
    dt = jax.nn.softplus(dt_raw.astype(jnp.float32).reshape(b, s, 2, SSM_HEADS)
                         + dt_bias.astype(jnp.float32))
    A = -jnp.exp(a_log.astype(jnp.float32))
    y_fwd = ssd_scan(xs, dt[:, :, 0], A[0], Bm, Cm)
    flip = lambda t: jnp.flip(t, axis=1)
    y_bwd = flip(ssd_scan(flip(xs), flip(dt[:, :, 1]), A[1], flip(Bm), flip(Cm)))
    y = y_fwd + y_bwd + d_skip.astype(jnp.float32)[:, None] * xs.astype(jnp.float32)
    y = y.reshape(b, s, D_INNER) * jax.nn.silu(z.astype(jnp.float32))
    return rms_norm(y, ssm_norm).astype(z.dtype)


def encoder_layer(x, ffn1_norm, ffn1_w_gate, ffn1_w_up, ffn1_w_down, mix_norm, w_in,
                  q_norm, k_norm, conv_w, conv_b, dt_bias, a_log, d_skip, ssm_norm,
                  w_attn_out, w_ssm_out, w_out, ffn2_norm, ffn2_w_gate, ffn2_w_up, ffn2_w_down):
    b, s, _ = x.shape
    x = x + 0.5 * swiglu(rms_norm(x, ffn1_norm), ffn1_w_gate, ffn1_w_up, ffn1_w_down)
    h = rms_norm(x, mix_norm)
    proj = h @ w_in
    cuts = np.cumsum([ATT_W, ATT_W, ATT_W, D_INNER, CONV_DIM, 2 * SSM_HEADS]).tolist()
    q, k, v, z, xbc, dt_raw, gate_raw = jnp.split(proj, cuts, axis=-1)
    hs = (b, s, ATT_HEADS, HEAD_DIM)
    y_att = attention_branch(q.reshape(hs), k.reshape(hs), v.reshape(hs), q_norm, k_norm) @ w_attn_out
    y_ssm = ssd_branch(z, xbc, dt_raw, conv_w, conv_b, dt_bias, a_log, d_skip, ssm_norm) @ w_ssm_out
    g_att, g_ssm = jnp.split(jax.nn.sigmoid(gate_raw), 2, axis=-1)
    x = x + (g_att * y_att + g_ssm * y_ssm) @ w_out
    x = x + 0.5 * swiglu(rms_norm(x, ffn2_norm), ffn2_w_gate, ffn2_w_up, ffn2_w_down)
    return x


def setup_inputs(seed: int = 0) -> dict:
    key = jax.random.key(seed)
    ks = jax.random.split(key, 26)
    f32 = jnp.float32

    def dense(k, shape, fan_in):
        return jax.random.normal(k, shape, f32) * (fan_in ** -0.5)

    def gain(k, shape):
        return 1.0 + 0.02 * jax.random.normal(k, shape, f32)

    L = DEPTH
    u = jax.random.uniform(ks[11], (L, 2, SSM_HEADS), f32)
    dt0 = jnp.exp(u * (math.log(0.1) - math.log(0.001)) + math.log(0.001))
    dt_bias = dt0 + jnp.log(-jnp.expm1(-dt0))
    a_log = jnp.log(jax.random.uniform(ks[12], (L, 2, SSM_HEADS), f32, 1.0, 16.0))
    return {
        'x_prompt': jax.random.normal(ks[0], (BATCH, SEQ, D_MODEL), f32),
        'x_sample': jax.random.normal(ks[1], (DEC_BATCH, DEC_SEQ, D_MODEL), f32),
        'ffn1_norm': gain(ks[2], (L, D_MODEL)),
        'ffn1_w_gate': dense(ks[3], (L, D_MODEL, D_FF), D_MODEL),
        'ffn1_w_up': dense(ks[4], (L, D_MODEL, D_FF), D_MODEL),
        'ffn1_w_down': dense(ks[5], (L, D_FF, D_MODEL), D_FF),
        'mix_norm': gain(ks[6], (L, D_MODEL)),
        'w_in': dense(ks[7], (L, D_MODEL, N_IN), D_MODEL),
        'q_norm': gain(ks[8], (L, HEAD_DIM)),
        'k_norm': gain(ks[9], (L, HEAD_DIM)),
        'conv_w': dense(ks[10], (L, CONV_WIDTH, CONV_DIM), CONV_WIDTH),
        'conv_b': 0.02 * jax.random.normal(ks[13], (L, CONV_DIM), f32),
        'dt_bias': dt_bias,
        'a_log': a_log,
        'd_skip': gain(ks[14], (L, SSM_HEADS)),
        'ssm_norm': gain(ks[15], (L, D_INNER)),
        'w_attn_out': dense(ks[16], (L, ATT_OUT, D_MODEL), ATT_OUT),
        'w_ssm_out': dense(ks[17], (L, D_INNER, D_MODEL), D_INNER),
        'w_out': dense(ks[18], (L, D_MODEL, D_MODEL), D_MODEL),
        'ffn2_norm': gain(ks[19], (L, D_MODEL)),
        'ffn2_w_gate': dense(ks[20], (L, D_MODEL, D_FF), D_MODEL),
        'ffn2_w_up': dense(ks[21], (L, D_MODEL, D_FF), D_MODEL),
        'ffn2_w_down': dense(ks[22], (L, D_FF, D_MODEL), D_FF),
    }


def reference(x_prompt, x_sample, ffn1_norm, ffn1_w_gate, ffn1_w_up, ffn1_w_down, mix_norm,
              w_in, q_norm, k_norm, conv_w, conv_b, dt_bias, a_log, d_skip, ssm_norm,
              w_attn_out, w_ssm_out, w_out, ffn2_norm, ffn2_w_gate, ffn2_w_up, ffn2_w_down):
    params = (ffn1_norm, ffn1_w_gate, ffn1_w_up, ffn1_w_down, mix_norm, w_in, q_norm, k_norm,
              conv_w, conv_b, dt_bias, a_log, d_skip, ssm_norm, w_attn_out, w_ssm_out, w_out,
              ffn2_norm, ffn2_w_gate, ffn2_w_up, ffn2_w_down)

    def trunk(x):
        for layer in range(DEPTH):
            x = encoder_layer(x, *[p[layer] for p in params])
        return x

    y_prompt = trunk(x_prompt)
    y_sample = trunk(x_sample)
    return (y_prompt, y_sample)
```

```python
import math
from contextlib import ExitStack
import numpy as np
import concourse.bass as bass
import concourse.mybir as mybir
from concourse.bass_utils import run_bass_kernel_spmd

F32 = mybir.dt.float32
BF16 = mybir.dt.bfloat16
AF = mybir.ActivationFunctionType
ALU = mybir.AluOpType

D = 4096
T = 4096
TT = 512
NTILE = T // TT
L_FULL = 4
N_IN = 17984
EPS = 1e-6

W_SPECS = [
    ("ffn1_norm", [D]), ("ffn1_w_gate", [D, 2048]), ("ffn1_w_up", [D, 2048]),
    ("ffn1_w_down", [2048, D]), ("mix_norm", [D]), ("w_in", [D, N_IN]),
    ("q_norm", [128]), ("k_norm", [128]), ("conv_w", [3, 3072]), ("conv_b", [3072]),
    ("dt_bias", [2, 32]), ("a_log", [2, 32]), ("d_skip", [32]), ("ssm_norm", [2048]),
    ("w_attn_out", [512, D]), ("w_ssm_out", [2048, D]), ("w_out", [D, D]),
    ("ffn2_norm", [D]), ("ffn2_w_gate", [D, 2048]), ("ffn2_w_up", [D, 2048]),
    ("ffn2_w_down", [2048, D]),
]

BIG_SEGS = {
    "ffn1_w_gate": [(0, 2048, 256)], "ffn1_w_up": [(0, 2048, 256)], "ffn1_w_down": [(0, D, 256)],
    "w_in": [(0, 9728, 256), (9728, 9792, 64), (9792, N_IN, 256)],
    "w_attn_out": [(0, D, 256)], "w_ssm_out": [(0, D, 256)], "w_out": [(0, D, 256)],
    "ffn2_w_gate": [(0, 2048, 256)], "ffn2_w_up": [(0, 2048, 256)], "ffn2_w_down": [(0, D, 256)],
}


class TWL:
    def __init__(self, segs, layer, K):
        self.segs, self.layer, self.K = segs, layer, K

    def block(self, c0, width):
        for (st, en, bw, ap) in self.segs:
            if st <= c0 < en:
                assert (c0 - st) % bw == 0 and width == bw, (c0, width, st, bw)
                return ap[self.layer, (c0 - st) // bw]
        raise AssertionError(c0)


class TW:
    def __init__(self, segs, K):
        self.segs, self.K = segs, K

    def __getitem__(self, layer):
        return TWL(self.segs, layer, self.K)


def tile_weight(W, st, en, bw):
    Lw, K, _ = W.shape
    KC = K // 128
    NB = (en - st) // bw
    return np.ascontiguousarray(W[:, :, st:en].reshape(Lw, KC, 128, NB, bw).transpose(0, 3, 2, 1, 4))


C_RM, C_ONES, C_MF, C_TF, C_MB, C_TB, C_AM, C_CV = 0, 128, 256, 384, 512, 640, 768, 1536
C_ID = 1538
NCST = 1666


class Sched:
    P = 12000
    K = 16
    U = 900

    def __init__(self, nc, es):
        self.nc, self.es = nc, es
        self.eng = dict(pe=nc.tensor, act=nc.scalar, dve=nc.vector, pool=nc.gpsimd, sp=nc.sync)
        self.nsig = {e: 0 for e in self.eng}
        self.esems = {e: [] for e in self.eng}
        self.waited = {}
        self.lastw = {}
        self.rd_e = {}
        self.rd_d = {}
        self.ndma = {}
        self.dsems = {}
        self.nsem = 0

    def _newsem(self, name):
        self.nsem += 1
        return self.es.enter_context(self.nc.semaphore(name))

    def _esem(self, e, ep):
        while len(self.esems[e]) <= ep:
            self.esems[e].append(self._newsem(f"e_{e}_{len(self.esems[e])}"))
        return self.esems[e][ep]

    def _dsem(self, q, i):
        st = i // (self.K * self.U)
        lst = self.dsems.setdefault(q, [])
        while len(lst) <= st:
            lst.append([self._newsem(f"d_{q}_{len(lst)}_{j}") for j in range(self.K)])
        slot = i % self.K
        val = 16 * ((i % (self.K * self.U)) // self.K + 1)
        return lst[st][slot], st, slot, val

    def _wait(self, e, tok):
        if tok[0] == 'e':
            _, e2, n = tok
            if e2 == e and e == 'pe':
                return
            ep, val = (n - 1) // self.P, (n - 1) % self.P + 1
            if self.waited.get((e, e2, 'ep'), -1) > ep:
                return
            if self.waited.get((e, e2, ep), 0) >= val:
                return
            self.eng[e].wait_ge(self._esem(e2, ep), val)
            self.waited[(e, e2, ep)] = val
            self.waited[(e, e2, 'ep')] = max(ep, self.waited.get((e, e2, 'ep'), -1))
        else:
            _, q, i = tok
            sem, st, slot, val = self._dsem(q, i)
            key = (e, 'd', q, st, slot)
            if self.waited.get(key, 0) >= val:
                return
            self.eng[e].wait_ge(sem, val)
            self.waited[key] = val

    def _deps(self, reads, writes):
        toks = []
        for k in reads:
            t = self.lastw.get(k)
            if t is not None:
                toks.append(t)
        for k in writes:
            t = self.lastw.get(k)
            if t is not None:
                toks.append(t)
            for e2, n in self.rd_e.get(k, {}).items():
                toks.append(('e', e2, n))
            for t in self.rd_d.get(k, ()):
                toks.append(t)
        return toks

    def _register(self, tok, reads, writes):
        for k in reads:
            if tok[0] == 'e':
                d = self.rd_e.setdefault(k, {})
                d[tok[1]] = max(d.get(tok[1], 0), tok[2])
            else:
                self.rd_d.setdefault(k, []).append(tok)
        for k in writes:
            self.lastw[k] = tok
            self.rd_e[k] = {}
            self.rd_d[k] = []

    def op(self, e, fn, reads=(), writes=(), signal=True):
        for t in self._deps(reads, writes):
            self._wait(e, t)
        ins = fn(self.eng[e])
        if signal:
            self.nsig[e] += 1
            n = self.nsig[e]
            ins.then_inc(self._esem(e, (n - 1) // self.P), 1)
            tok = ('e', e, n)
        else:
            tok = ('e', e, self.nsig[e] + 1)
        self._register(tok, reads, writes)
        return ins

    def dma(self, q, out, in_, reads=(), writes=(), **kw):
        i = self.ndma.get(q, 0)
        toks = self._deps(reads, writes)
        if i >= self.K:
            toks.append(('d', q, i - self.K))
        for t in toks:
            self._wait(q, t)
        sem, st, slot, val = self._dsem(q, i)
        self.eng[q].dma_start(out=out, in_=in_, **kw).then_inc(sem, 16)
        self.ndma[q] = i + 1
        self._register(('d', q, i), reads, writes)

    def finish(self, out_keys):
        for k in out_keys:
            t = self.lastw.get(k)
            if t is not None:
                self._wait('sp', t)
        for q, n in self.ndma.items():
            for i in range(max(0, n - self.K), n):
                self._wait('sp', ('d', q, i))


def build_nc(n_layers=L_FULL, dbg=False, stages=("p1", "att", "ssd", "p3")):
    nc = bass.Bass("TRN2", target_bir_lowering=False)
    x_in = nc.dram_tensor("x", [T, D], F32, kind="ExternalInput").ap()
    w = {}
    for name, shp in W_SPECS:
        if name in BIG_SEGS:
            K = shp[0]
            segs = []
            for si, (st, en, bw) in enumerate(BIG_SEGS[name]):
                ap = nc.dram_tensor(f"{name}_t{si}", [n_layers, (en - st) // bw, 128, K // 128, bw], F32, kind="ExternalInput").ap()
                segs.append((st, en, bw, ap))
            w[name] = TW(segs, K)
        else:
            w[name] = nc.dram_tensor(name, [n_layers] + shp, F32, kind="ExternalInput").ap()
    cst_d = nc.dram_tensor("cst", [128, NCST], F32, kind="ExternalInput").ap()
    cos_d = nc.dram_tensor("cosT", [128, T], F32, kind="ExternalInput").ap()
    sin_d = nc.dram_tensor("sinT", [128, T], F32, kind="ExternalInput").ap()
    y = nc.dram_tensor("y", [T, D], F32, kind="ExternalOutput").ap()
    sk = "ExternalOutput" if dbg else "Internal"
    qk_d = nc.dram_tensor("qk_s", [24, 128, T], BF16, kind=sk).ap()
    v_d = nc.dram_tensor("v_s", [T, 1536], BF16, kind=sk).ap()
    z_d = nc.dram_tensor("z_s", [T, 2048], F32, kind=sk).ap()
    xbc_d = nc.dram_tensor("xbc_s", [T, 3072], F32, kind=sk).ap()
    dtr_d = nc.dram_tensor("dtr_s", [T, 64], F32, kind=sk).ap()
    g_d = nc.dram_tensor("g_s", [64, 128, T], BF16, kind=sk).ap()
    o_d = nc.dram_tensor("o_s", [4, 128, T], BF16, kind=sk).ap()
    y1_d = nc.dram_tensor("y1_s", [T, 2048], F32, kind=sk).ap()
    s_d = nc.dram_tensor("s_s", [16, 128, T], BF16, kind=sk).ap()

    es = ExitStack()
    with es:
        S = Sched(nc, es)

        def sb(name, shape, dt):
            return es.enter_context(nc.sbuf_tensor("sb_" + name, shape, dt))

        psb = [es.enter_context(nc.psum_tensor(f"ps{i}", [128, 512], F32)) for i in range(6)]
        pst = [es.enter_context(nc.psum_tensor(f"pst{i}", [128, 1024], BF16)) for i in range(2)]
        ps_rr = [0]
        pst_rr = [0]

        def next_ps():
            i = ps_rr[0] % 6
            ps_rr[0] += 1
            return psb[i], f"ps{i}"

        def next_pst():
            i = pst_rr[0] % 2
            pst_rr[0] += 1
            return pst[i], f"pst{i}"

        cst = sb("cst", [128, 256], F32)
        S.dma('sp', cst[:], cst_d[:, 0:256], writes=["cst"])
        idf = sb("idf", [128, 128], F32)
        S.dma('sp', idf[:], cst_d[:, C_ID:C_ID + 128], writes=["idf"])
        ones_bf = sb("ones_bf", [128, 128], BF16)
        S.op('dve', lambda v: v.tensor_copy(out=ones_bf[:], in_=cst[:, C_ONES:C_ONES + 128]), reads=["cst"], writes=["ones_bf"])
        ident = sb("ident", [128, 128], BF16)
        S.op('dve', lambda v: v.tensor_copy(out=ident[:], in_=idf[:]), reads=["idf"], writes=["ident"])
        epst = sb("epst", [128, 1], F32)
        S.op('dve', lambda v: v.memset(epst[:], EPS), writes=["epst"])
        Rm = cst[:, C_RM:C_RM + 128]
        ones_f = cst[:, C_ONES:C_ONES + 128]

        NW = 3
        wbuf = sb("wbuf", [128, NW, 32, 256], BF16)
        w_rr = [0]

        def wload(wap, c0, width):
            K = wap.K
            KC = K // 128
            s = w_rr[0] % NW
            w_rr[0] += 1
            key = f"w{s}"
            S.dma('pool', wbuf[:, s, 0:KC, 0:width], wap.block(c0, width), writes=[key])
            return wbuf[:, s], key, KC

        def mm_group(ps_ap, pskey, KC, lhs_fn, rhs_fn, rkeys):
            for kc in range(KC):
                S.op('pe', (lambda t, kc=kc: t.matmul(ps_ap, lhsT=lhs_fn(kc), rhs=rhs_fn(kc),
                                                      start=(kc == 0), stop=(kc == KC - 1))),
                     reads=rkeys, writes=[pskey], signal=(kc == KC - 1))

        def gemm_fm(wap, c0, ncols, hT, hkey, evac):
            for b0 in range(c0, c0 + ncols, 256):
                wd = min(256, c0 + ncols - b0)
                wt, wkey, KC = wload(wap, b0, wd)
                for ci in range(wd // 128):
                    ps, pskey = next_ps()
                    mm_group(ps[:, 0:TT], pskey, KC,
                             lambda kc, ci=ci, wt=wt: wt[:, kc, ci * 128:(ci + 1) * 128],
                             lambda kc: hT[:, kc, 0:TT], [wkey, hkey])
                    evac(ps, pskey, b0 + ci * 128)

        def gemm_tm(wap, c0, ncols, hT, hkey, evac, blk=256):
            for b0 in range(c0, c0 + ncols, blk):
                wd = min(blk, c0 + ncols - b0)
                wt, wkey, KC = wload(wap, b0, wd)
                for b in range(TT // 128):
                    ps, pskey = next_ps()
                    mm_group(ps[:, 0:wd], pskey, KC,
                             lambda kc, b=b: hT[:, kc, b * 128:(b + 1) * 128],
                             lambda kc, wt=wt, wd=wd: wt[:, kc, 0:wd], [wkey, hkey])
                    evac(ps, pskey, b, b0, wd)

        def make_fns(xt, hT, hid, hb, gbc, ss, stg, next_stg, layer):
            def norm_to_hT(gname):
                S.dma('sp', gbc[:], w[gname][layer].partition_broadcast(128), writes=["gbc"])
                for b in range(4):
                    S.op('dve', lambda v, b=b: v.scalar_tensor_tensor(
                        out=hb[:], in0=xt[:, b, :], scalar=1.0, in1=xt[:, b, :],
                        op0=ALU.mult, op1=ALU.mult, accum_out=ss[:, 0:1]),
                        reads=["xt"], writes=["hb", "ss"])
                    S.op('act', lambda a: a.activation(out=ss[:, 1:2], in_=ss[:, 0:1], func=AF.Sqrt,
                                                       bias=epst[:, 0:1], scale=1.0 / D),
                         reads=["ss", "epst"], writes=["ss"])
                    S.op('dve', lambda v: v.reciprocal(out=ss[:, 2:3], in_=ss[:, 1:2]), reads=["ss"], writes=["ss"])
                    S.op('dve', lambda v, b=b: v.scalar_tensor_tensor(
                        out=hb[:], in0=xt[:, b, :], scalar=ss[:, 2:3], in1=gbc[:],
                        op0=ALU.mult, op1=ALU.mult), reads=["xt", "ss", "gbc"], writes=["hb"])
                    for k0 in range(0, 32, 8):
                        pt, ptkey = next_pst()
                        for j in range(8):
                            kc = k0 + j
                            S.op('pe', lambda t, kc=kc, j=j, pt=pt: t.transpose(
                                out=pt[:, j * 128:(j + 1) * 128], in_=hb[:, kc * 128:(kc + 1) * 128], identity=ident[:]),
                                reads=["hb", "ident"], writes=[ptkey], signal=(j == 7))
                        eng = 'act' if (k0 // 8) % 2 == 0 else 'pool'
                        if eng == 'act':
                            S.op('act', lambda a, k0=k0, b=b, pt=pt: a.copy(
                                out=hT[:, k0:k0 + 8, b * 128:(b + 1) * 128],
                                in_=pt[:, :].rearrange("p (j t) -> p j t", j=8)),
                                reads=[ptkey], writes=["hT"])
                        else:
                            S.op('dve', lambda v, k0=k0, b=b, pt=pt: v.tensor_copy(
                                out=hT[:, k0:k0 + 8, b * 128:(b + 1) * 128],
                                in_=pt[:, :].rearrange("p (j t) -> p j t", j=8)),
                                reads=[ptkey], writes=["hT"])

            def ffn(wg, wu, wdn):
                for nb in range(0, 2048, 128):
                    pass
                for b0 in range(0, 2048, 256):
                    wgt, wgk, KC = wload(wg, b0, 256)
                    wut, wuk, _ = wload(wu, b0, 256)
                    for ci in range(2):
                        psg, pgk = next_ps()
                        mm_group(psg[:, 0:TT], pgk, KC,
                                 lambda kc, ci=ci, wgt=wgt: wgt[:, kc, ci * 128:(ci + 1) * 128],
                                 lambda kc: hT[:, kc, 0:TT], [wgk, "hT"])
                        psu, puk = next_ps()
                        mm_group(psu[:, 0:TT], puk, KC,
                                 lambda kc, ci=ci, wut=wut: wut[:, kc, ci * 128:(ci + 1) * 128],
                                 lambda kc: hT[:, kc, 0:TT], [wuk, "hT"])
                        si = next_stg()
                        S.op('act', lambda a, psg=psg, si=si: a.activation(out=stg[:, si, :], in_=psg[:, 0:TT], func=AF.Silu),
                             reads=[pgk], writes=[f"stg{si}"])
                        hc = b0 // 128 + ci
                        S.op('dve', lambda v, psu=psu, si=si, hc=hc: v.tensor_tensor(
                            out=hid[:, hc, :], in0=stg[:, si, :], in1=psu[:, 0:TT], op=ALU.mult),
                            reads=[puk, f"stg{si}"], writes=["hid"])

                def ev_down(ps, pskey, b, c0, wd):
                    S.op('dve', lambda v: v.scalar_tensor_tensor(
                        out=xt[:, b, c0:c0 + wd], in0=ps[:, 0:wd], scalar=0.5, in1=xt[:, b, c0:c0 + wd],
                        op0=ALU.mult, op1=ALU.add), reads=[pskey, "xt"], writes=["xt"])
                gemm_tm(wdn, 0, D, hid, "hid", ev_down)


            return norm_to_hT, ffn

        for layer in range(n_layers):
            x_src = x_in if layer == 0 else y
            if "p1" in stages:
              with ExitStack() as es1:
                def sb1(name, shape, dt):
                    return es1.enter_context(nc.sbuf_tensor(f"sb_{name}_{layer}", shape, dt))
                xt = sb1("xt", [128, 4, D], F32)
                hT = sb1("hT", [128, 32, TT], BF16)
                hid = sb1("hid", [128, 16, TT], BF16)
                hb = sb1("hb", [128, D], BF16)
                gbc = sb1("gbc", [128, D], F32)
                ss = sb1("ss", [128, 8], F32)
                cosS = sb1("cosS", [128, TT], F32)
                sinS = sb1("sinS", [128, TT], F32)
                stg = sb1("stg", [128, 4, 512], F32)
                stgb = sb1("stgb", [128, 4, 512], BF16)
                qkn = sb1("qkn", [128, 2], F32)
                dtst = sb1("dtst", [128, 4, 64], F32)
                stg_rr = [0]

                def next_stg():
                    i = stg_rr[0] % 4
                    stg_rr[0] += 1
                    return i

                S.dma('sp', qkn[:, 0:1], w["q_norm"][layer].rearrange("(p o) -> p o", o=1), writes=["qkn"])
                S.dma('sp', qkn[:, 1:2], w["k_norm"][layer].rearrange("(p o) -> p o", o=1), writes=["qkn"])

                norm_to_hT, ffn = make_fns(xt, hT, hid, hb, gbc, ss, stg, next_stg, layer)

                for tile in range(NTILE):
                    t0 = tile * TT
                    S.dma('sp', xt[:], x_src[t0:t0 + TT, :].rearrange("(b p) f -> p b f", p=128),
                          reads=["ydram"], writes=["xt"])
                    S.dma('sp', cosS[:], cos_d[:, t0:t0 + TT], writes=["cosS"])
                    S.dma('sp', sinS[:], sin_d[:, t0:t0 + TT], writes=["sinS"])
                    norm_to_hT("ffn1_norm")
                    ffn(w["ffn1_w_gate"][layer], w["ffn1_w_up"][layer], w["ffn1_w_down"][layer])
                    norm_to_hT("mix_norm")
                    win = w["w_in"][layer]

                    def ev_qk(ps, pskey, col):
                        hd = col // 128
                        isk = 1 if hd >= 12 else 0
                        si = next_stg()
                        sj = next_stg()
                        S.op('act', lambda a: a.copy(out=stg[:, si, :], in_=ps[:, 0:TT]), reads=[pskey], writes=[f"stg{si}"])
                        S.op('act', lambda a: a.activation(out=stg[:, sj, :], in_=ps[:, 0:TT], func=AF.Square),
                             reads=[pskey], writes=[f"stg{sj}"])
                        p2, p2k = next_ps()
                        S.op('pe', lambda t: t.matmul(p2[:, 0:TT], lhsT=ones_f, rhs=stg[:, sj, :], start=True, stop=True),
                             reads=["cst", f"stg{sj}"], writes=[p2k])
                        S.op('act', lambda a: a.activation(out=stg[:, sj, :], in_=p2[:, 0:TT], func=AF.Sqrt,
                                                           bias=epst[:, 0:1], scale=1.0 / 128),
                             reads=[p2k, "epst"], writes=[f"stg{sj}"])
                        S.op('dve', lambda v: v.reciprocal(out=stg[:, sj, :], in_=stg[:, sj, :]),
                             reads=[f"stg{sj}"], writes=[f"stg{sj}"])
                        S.op('dve', lambda v: v.scalar_tensor_tensor(
                            out=stg[:, si, :], in0=stg[:, si, :], scalar=qkn[:, isk:isk + 1], in1=stg[:, sj, :],
                            op0=ALU.mult, op1=ALU.mult), reads=[f"stg{si}", f"stg{sj}", "qkn"], writes=[f"stg{si}"])
                        p3, p3k = next_ps()
                        S.op('pe', lambda t: t.matmul(p3[:, 0:TT], lhsT=Rm, rhs=stg[:, si, :], start=True, stop=True),
                             reads=["cst", f"stg{si}"], writes=[p3k])
                        S.op('dve', lambda v: v.tensor_tensor(out=stg[:, sj, :], in0=p3[:, 0:TT], in1=sinS[:], op=ALU.mult),
                             reads=[p3k, "sinS"], writes=[f"stg{sj}"])
                        S.op('dve', lambda g: g.tensor_tensor(out=stg[:, si, :], in0=stg[:, si, :], in1=cosS[:], op=ALU.mult),
                             reads=[f"stg{si}", "cosS"], writes=[f"stg{si}"])
                        S.op('dve', lambda v: v.tensor_tensor(out=stgb[:, si, :], in0=stg[:, si, :], in1=stg[:, sj, :], op=ALU.add),
                             reads=[f"stg{si}", f"stg{sj}"], writes=[f"stgb{si}"])
                        S.dma('sp', qk_d[hd, :, t0:t0 + TT], stgb[:, si, :], reads=[f"stgb{si}"], writes=["qk_d"])
                    gemm_fm(win, 0, 3072, hT, "hT", ev_qk)

                    def ev_v(ps, pskey, b, c0, wd):
                        si = next_stg()
                        S.op('act', lambda a: a.copy(out=stgb[:, si, 0:wd], in_=ps[:, 0:wd]), reads=[pskey], writes=[f"stgb{si}"])
                        S.dma('sp', v_d[t0 + b * 128:t0 + (b + 1) * 128, c0 - 3072:c0 - 3072 + wd], stgb[:, si, 0:wd],
                              reads=[f"stgb{si}"], writes=["v_d"])
                    gemm_tm(win, 3072, 1536, hT, "hT", ev_v)

                    def ev_zx(ps, pskey, b, c0, wd):
                        si = next_stg()
                        if si % 2 == 0:
                            S.op('act', lambda a: a.copy(out=stg[:, si, 0:wd], in_=ps[:, 0:wd]), reads=[pskey], writes=[f"stg{si}"])
                        else:
                            S.op('dve', lambda v: v.tensor_copy(out=stg[:, si, 0:wd], in_=ps[:, 0:wd]), reads=[pskey], writes=[f"stg{si}"])
                        if c0 < 6656:
                            dst = z_d[t0 + b * 128:t0 + (b + 1) * 128, c0 - 4608:c0 - 4608 + wd]
                            dk = "z_d"
                        else:
                            dst = xbc_d[t0 + b * 128:t0 + (b + 1) * 128, c0 - 6656:c0 - 6656 + wd]
                            dk = "xbc_d"
                        S.dma('sp', dst, stg[:, si, 0:wd], reads=[f"stg{si}"], writes=[dk])
                    gemm_tm(win, 4608, 2048 + 3072, hT, "hT", ev_zx)

                    def ev_dt(ps, pskey, b, c0, wd):
                        S.op('act', lambda a: a.copy(out=dtst[:, b, :], in_=ps[:, 0:64]), reads=[pskey], writes=[f"dtst{b}"])
                        S.dma('sp', dtr_d[t0 + b * 128:t0 + (b + 1) * 128, :], dtst[:, b, :], reads=[f"dtst{b}"], writes=["dtr_d"])
                    gemm_tm(win, 9728, 64, hT, "hT", ev_dt)

                    def ev_g(ps, pskey, col):
                        si = next_stg()
                        S.op('act', lambda a: a.activation(out=stgb[:, si, :], in_=ps[:, 0:TT], func=AF.Sigmoid),
                             reads=[pskey], writes=[f"stgb{si}"])
                        S.dma('sp', g_d[(col - 9792) // 128, :, t0:t0 + TT], stgb[:, si, :], reads=[f"stgb{si}"], writes=["g_d"])
                    gemm_fm(win, 9792, 8192, hT, "hT", ev_g)

                    S.dma('sp', y[t0:t0 + TT, :].rearrange("(b p) f -> p b f", p=128), xt[:], reads=["xt"], writes=["ydram"])

            if "att" in stages:
              with ExitStack() as es2:
                def sb2(name, shape, dt):
                    return es2.enter_context(nc.sbuf_tensor(f"sa_{name}_{layer}", shape, dt))
                QS = [sb2(f"QS{g}", [128, T], BF16) for g in range(3)]
                KS = [sb2(f"KS{g}", [128, T], BF16) for g in range(3)]
                VS = [sb2(f"VS{g}", [128, 32, 128], BF16) for g in range(3)]
                accn = sb2("accn", [128, T], F32)
                accd = sb2("accd", [128, T], F32)
                obf = sb2("obf", [128, T], BF16)
                tmpq = sb2("tmpq", [128, T], BF16)
                am = sb2("am", [128, 768], F32)
                pex = sb2("pex", [128, 2, 256], F32)
                pbf = sb2("pbf", [128, 2, 256], BF16)
                S.dma('sp', am[:], cst_d[:, C_AM:C_AM + 768], writes=["am"])
                unit = 0
                for j in range(4):
                    for g, d in enumerate((1, 4, 16)):
                        n = T // d
                        nch = n // 128
                        for (dst, dkey, hidx) in ((QS[g], f"QS{g}", g * 4 + j), (KS[g], f"KS{g}", 12 + g * 4 + j)):
                            if d == 1:
                                S.dma('sp', dst[:], qk_d[hidx], reads=["qk_d"], writes=[dkey])
                            else:
                                S.dma('sp', tmpq[:], qk_d[hidx], reads=["qk_d"], writes=["tmpq"])
                                S.op('pool', lambda e, dst=dst, d=d: e.tensor_copy(
                                    out=dst[:, :].rearrange("p (dd m) -> p dd m", dd=d),
                                    in_=tmpq[:, :].rearrange("p (m dd) -> p dd m", dd=d)),
                                    reads=["tmpq"], writes=[dkey])
                        vsrc = v_d[:, (g * 4 + j) * 128:(g * 4 + j + 1) * 128].rearrange("(m dd) f -> dd m f", dd=d)
                        for r in range(d):
                            S.dma('sp', VS[g][:, r * nch:(r + 1) * nch, :],
                                  vsrc[r].rearrange("(c i) f -> i c f", i=128), reads=["v_d"], writes=[f"VS{g}"])
                    S.op('pool', lambda e: e.memset(accn[:], 0.0), writes=["accn"])
                    S.op('pool', lambda e: e.memset(accd[:], 0.0), writes=["accd"])
                    for g, d in enumerate((1, 4, 16)):
                        n = T // d
                        nch = n // 128
                        cb = nch // 2
                        Qv = QS[g][:, :].rearrange("p (dd m) -> p dd m", dd=d)
                        Kv = KS[g][:, :].rearrange("p (dd m) -> p dd m", dd=d)
                        An = accn[:, :].rearrange("p (m dd) -> p dd m", dd=d)
                        Ad = accd[:, :].rearrange("p (m dd) -> p dd m", dd=d)
                        for r in range(d):
                            for c in range(nch):
                                qlo = max(0, 128 * c - 64)
                                qhi = min(n, 128 * c + 192)
                                nq = qhi - qlo
                                j0 = qlo - (128 * c - 64)
                                mi = 1 if c == cb else (2 if c == cb - 1 else 0)
                                mask = am[:, mi * 256 + j0:mi * 256 + j0 + nq]
                                sl = unit % 2
                                unit += 1
                                ps, pk = next_ps()
                                S.op('pe', lambda t, ps=ps, c=c, r=r, qlo=qlo, qhi=qhi, nq=nq, Kv=Kv, Qv=Qv: t.matmul(
                                    ps[:, 0:nq], lhsT=Kv[:, r, 128 * c:128 * c + 128], rhs=Qv[:, r, qlo:qhi],
                                    start=True, stop=True), reads=[f"KS{g}", f"QS{g}"], writes=[pk])
                                S.op('act', lambda a, ps=ps, sl=sl, nq=nq: a.activation(
                                    out=pex[:, sl, 0:nq], in_=ps[:, 0:nq], func=AF.Exp, scale=128.0 ** -0.5),
                                    reads=[pk], writes=[f"pex{sl}"])
                                S.op('pool', lambda e, sl=sl, nq=nq, mask=mask: e.tensor_tensor(
                                    out=pbf[:, sl, 0:nq], in0=pex[:, sl, 0:nq], in1=mask, op=ALU.mult),
                                    reads=[f"pex{sl}", "am"], writes=[f"pbf{sl}"])
                                psn, pnk = next_ps()
                                psd, pdk = next_ps()
                                bi = r * nch + c
                                S.op('pe', lambda t, psn=psn, bi=bi, sl=sl, nq=nq, g=g: t.matmul(
                                    psn[:, 0:nq], lhsT=VS[g][:, bi, :], rhs=pbf[:, sl, 0:nq], start=True, stop=True),
                                    reads=[f"VS{g}", f"pbf{sl}"], writes=[pnk])
                                S.op('pe', lambda t, psd=psd, sl=sl, nq=nq: t.matmul(
                                    psd[:, 0:nq], lhsT=ones_bf[:], rhs=pbf[:, sl, 0:nq], start=True, stop=True),
                                    reads=["ones_bf", f"pbf{sl}"], writes=[pdk])
                                S.op('dve', lambda v, psn=psn, r=r, qlo=qlo, qhi=qhi, nq=nq, An=An: v.tensor_tensor(
                                    out=An[:, r, qlo:qhi], in0=An[:, r, qlo:qhi], in1=psn[:, 0:nq], op=ALU.add),
                                    reads=[pnk, "accn"], writes=["accn"])
                                S.op('dve', lambda v, psd=psd, r=r, qlo=qlo, qhi=qhi, nq=nq, Ad=Ad: v.tensor_tensor(
                                    out=Ad[:, r, qlo:qhi], in0=Ad[:, r, qlo:qhi], in1=psd[:, 0:nq], op=ALU.add),
                                    reads=[pdk, "accd"], writes=["accd"])
                    S.op('dve', lambda v: v.reciprocal(out=accd[:], in_=accd[:]), reads=["accd"], writes=["accd"])
                    S.op('dve', lambda v: v.tensor_tensor(out=obf[:], in0=accn[:], in1=accd[:], op=ALU.mult),
                         reads=["accn", "accd"], writes=["obf"])
                    S.dma('sp', o_d[j], obf[:], reads=["obf"], writes=["o_d"])

            if "ssd" in stages:
              with ExitStack() as es3:
                def sb3(name, shape, dt):
                    return es3.enter_context(nc.sbuf_tensor(f"ss_{name}_{layer}", shape, dt))
                sm = sb3("sm", [128, 512], F32)
                S.dma('sp', sm[:], cst_d[:, C_MF:C_MF + 512], writes=["sm"])
                cv = sb3("cv", [128, 2], F32)
                S.dma('sp', cv[:], cst_d[:, C_CV:C_CV + 2], writes=["cv"])
                cw = wbuf[:, :, :, :].rearrange("p a b c -> p (a b c)").bitcast(F32).rearrange("p (a c) -> p a c", a=4)
                WK = ["w0", "w1", "w2"]
                S.dma('sp', cw[:, 0:3, :], w["conv_w"][layer].partition_broadcast(128), writes=WK)
                S.dma('sp', cw[:, 3, :], w["conv_b"][layer].partition_broadcast(128), writes=WK)
                dtb = sb3("dtb", [128, 64], F32)
                Abc = sb3("Abc", [128, 64], F32)
                dsk = sb3("dsk", [128, 32], F32)
                gn = sb3("gn", [128, 2048], F32)
                S.dma('sp', dtb[:], w["dt_bias"][layer].rearrange("a b -> (a b)").partition_broadcast(128), writes=["dtb"])
                S.dma('sp', Abc[:], w["a_log"][layer].rearrange("a b -> (a b)").partition_broadcast(128), writes=["Abc"])
                S.dma('sp', dsk[:], w["d_skip"][layer].partition_broadcast(128), writes=["dsk"])
                S.dma('sp', gn[:], w["ssm_norm"][layer].partition_broadcast(128), writes=["gn"])
                S.op('act', lambda a: a.activation(out=Abc[:], in_=Abc[:], func=AF.Exp), reads=["Abc"], writes=["Abc"])
                S.op('dve', lambda v: v.tensor_scalar(out=Abc[:], in0=Abc[:], scalar1=-1.0, scalar2=None, op0=ALU.mult),
                     reads=["Abc"], writes=["Abc"])
                tri4 = sb3("tri4", [128, 2, 4, 128], F32)
                for dd in range(2):
                    for rep in range(4):
                        S.op('dve', lambda v, dd=dd, rep=rep: v.tensor_copy(
                            out=tri4[:, dd, rep, :], in_=sm[:, 128 + 256 * dd:256 + 256 * dd]), reads=["sm"], writes=["tri4"])
                xm = sb3("xm", [128, 3072], F32)
                x0 = sb3("x0", [128, 3072], F32)
                xp = sb3("xp", [128, 3072], F32)
                xbf = sb3("xbf", [128, 3072], BF16)
                BCT = sb3("BCT", [128, 8, 128], BF16)
                cbm = sb3("cbm", [128, 4, 512], F32)
                Rt = sb3("Rt", [128, 32, 128], F32)
                Et = sb3("Et", [128, 2, 512], F32)
                sct = sb3("sct", [128, 2, 512], BF16)
                xdt = sb3("xdt", [128, 2048], BF16)
                xw = sb3("xw", [128, 2048], BF16)
                yacc = sb3("yacc", [128, 2048], F32)
                tmpy = sb3("tmpy", [128, 2, 512], F32)
                ST = sb3("ST", [128, 2048], F32)
                STb = sb3("STb", [128, 2048], BF16)
                zt = sb3("zt", [128, 2048], F32)
                ynb = sb3("ynb", [128, 2048], BF16)
                sTt = sb3("sTt", [128, 16, 128], BF16)
                dtt = sb3("dtt", [128, 8, 32], F32)
                dtr = sb3("dtr", [128, 64], F32)
                ssq = sb3("ssq", [128, 4], F32)
                xs3 = x0[:, 0:2048].rearrange("p (h q) -> p h q", q=64)

                def bc64(ap32):
                    return ap32.unsqueeze(2).to_broadcast([128, 32, 64])

                for dd in range(2):
                    Mk = sm[:, 256 * dd:256 * dd + 128]
                    Tri = sm[:, 256 * dd + 128:256 * dd + 256]
                    S.op('pool', lambda e: e.memset(ST[:], 0.0), writes=["ST"])
                    S.op('pool', lambda e: e.memset(STb[:], 0.0), writes=["STb"])
                    order = list(range(32)) if dd == 0 else list(range(31, -1, -1))
                    for c in order:
                        t0 = c * 128
                        S.dma('sp', x0[:], xbc_d[t0:t0 + 128, :], reads=["xbc_d"], writes=["x0"])
                        if c == 0:
                            S.op('pool', lambda e: e.memset(xm[0:1, :], 0.0), writes=["xm"])
                            S.dma('sp', xm[1:128, :], xbc_d[0:127, :], reads=["xbc_d"], writes=["xm"])
                        else:
                            S.dma('sp', xm[:], xbc_d[t0 - 1:t0 + 127, :], reads=["xbc_d"], writes=["xm"])
                        if c == 31:
                            S.op('pool', lambda e: e.memset(xp[:], 0.0), writes=["xp"])
                            S.dma('sp', xp[0:127, :], xbc_d[t0 + 1:t0 + 128, :], reads=["xbc_d"], writes=["xp"])
                        else:
                            S.dma('sp', xp[:], xbc_d[t0 + 1:t0 + 129, :], reads=["xbc_d"], writes=["xp"])
                        S.dma('sp', dtr[:], dtr_d[t0:t0 + 128, :], reads=["dtr_d"], writes=["dtr"])
                        if dd == 1:
                            S.dma('sp', zt[:], z_d[t0:t0 + 128, :], reads=["z_d"], writes=["zt"])
                            S.dma('sp', yacc[:], y1_d[t0:t0 + 128, :], reads=["y1_d"], writes=["yacc"])
                        if c == 16:
                            S.op('dve', lambda v: v.tensor_scalar(out=xm[0:1, :], in0=xm[0:1, :], scalar1=cv[0:1, 0:1],
                                                                  scalar2=None, op0=ALU.mult), reads=["xm", "cv"], writes=["xm"])
                        if c == 15:
                            S.op('dve', lambda v: v.tensor_scalar(out=xp[:], in0=xp[:], scalar1=cv[:, 1:2],
                                                                  scalar2=None, op0=ALU.mult), reads=["xp", "cv"], writes=["xp"])
                        if (dd == 0 and c == 16) or (dd == 1 and c == 15):
                            S.op('dve', lambda v: v.tensor_scalar(out=ST[:], in0=ST[:], scalar1=cv[:, 0:1], scalar2=None,
                                                                  op0=ALU.mult), reads=["ST", "cv"], writes=["ST"])
                            S.op('act', lambda a: a.copy(out=STb[:], in_=ST[:]), reads=["ST"], writes=["STb"])
                        S.op('pool', lambda e: e.tensor_tensor(out=xm[:], in0=xm[:], in1=cw[:, 0, :], op=ALU.mult), reads=["xm"] + WK, writes=["xm"])
                        S.op('dve', lambda v: v.tensor_tensor(out=x0[:], in0=x0[:], in1=cw[:, 1, :], op=ALU.mult), reads=["x0"] + WK, writes=["x0"])
                        S.op('pool', lambda e: e.tensor_tensor(out=xp[:], in0=xp[:], in1=cw[:, 2, :], op=ALU.mult), reads=["xp"] + WK, writes=["xp"])
                        S.op('dve', lambda v: v.tensor_tensor(out=x0[:], in0=x0[:], in1=xm[:], op=ALU.add), reads=["x0", "xm"], writes=["x0"])
                        S.op('pool', lambda e: e.tensor_tensor(out=xp[:], in0=xp[:], in1=cw[:, 3, :], op=ALU.add), reads=["xp"] + WK, writes=["xp"])
                        S.op('dve', lambda v: v.tensor_tensor(out=x0[:], in0=x0[:], in1=xp[:], op=ALU.add), reads=["x0", "xp"], writes=["x0"])
                        S.op('act', lambda a: a.activation(out=x0[:], in_=x0[:], func=AF.Silu), reads=["x0"], writes=["x0"])
                        S.op('act', lambda a: a.copy(out=xbf[:], in_=x0[:]), reads=["x0"], writes=["xbf"])
                        pt, ptk = next_pst()
                        for i in range(8):
                            S.op('pe', lambda t, i=i, pt=pt: t.transpose(out=pt[:, i * 128:(i + 1) * 128],
                                                                         in_=xbf[:, 2048 + i * 128:2048 + (i + 1) * 128], identity=ident[:]),
                                 reads=["xbf", "ident"], writes=[ptk], signal=(i == 7))
                        S.op('act', lambda a, pt=pt: a.copy(out=BCT[:, :, :], in_=pt[:, :].rearrange("p (j t) -> p j t", j=8)),
                             reads=[ptk], writes=["BCT"])
                        for g in range(4):
                            ps, pk = next_ps()
                            for rep in range(4):
                                S.op('pe', lambda t, ps=ps, g=g, rep=rep: t.matmul(ps[:, rep * 128:(rep + 1) * 128], lhsT=BCT[:, g, :],
                                                                                   rhs=BCT[:, 4 + g, :], start=True, stop=True),
                                     reads=["BCT"], writes=[pk], signal=(rep == 3))
                            S.op('dve', lambda v, ps=ps, g=g: v.tensor_tensor(
                                out=cbm[:, g, :], in0=ps[:, 0:512], in1=tri4[:, dd, :, :].rearrange("p a b -> p (a b)"), op=ALU.mult),
                                reads=[pk, "tri4"], writes=["cbm"])
                        dsl = slice(dd * 32, dd * 32 + 32)
                        S.op('dve', lambda v: v.tensor_tensor(out=dtt[:, 0, :], in0=dtr[:, dsl], in1=dtb[:, dsl], op=ALU.add),
                             reads=["dtr", "dtb"], writes=["dtt"])
                        S.op('act', lambda a: a.activation(out=dtt[:, 0, :], in_=dtt[:, 0, :], func=AF.Exp), reads=["dtt"], writes=["dtt"])
                        S.op('dve', lambda v: v.tensor_scalar(out=dtt[:, 0, :], in0=dtt[:, 0, :], scalar1=1.0, scalar2=None, op0=ALU.add),
                             reads=["dtt"], writes=["dtt"])
                        S.op('act', lambda a: a.activation(out=dtt[:, 0, :], in_=dtt[:, 0, :], func=AF.Ln), reads=["dtt"], writes=["dtt"])
                        S.op('dve', lambda v: v.tensor_tensor(out=dtt[:, 1, :], in0=dtt[:, 0, :], in1=Abc[:, dsl], op=ALU.mult),
                             reads=["dtt", "Abc"], writes=["dtt"])
                        psa, pak = next_ps()
                        S.op('pe', lambda t: t.matmul(psa[:, 0:32], lhsT=Tri, rhs=dtt[:, 1, :], start=True, stop=True),
                             reads=["sm", "dtt"], writes=[pak])
                        S.op('pe', lambda t: t.matmul(psa[:, 32:64], lhsT=ones_f, rhs=dtt[:, 1, :], start=True, stop=True),
                             reads=["cst", "dtt"], writes=[pak])
                        S.op('act', lambda a: a.copy(out=dtt[:, 2, :], in_=psa[:, 0:32]), reads=[pak], writes=["dtt"])
                        S.op('act', lambda a: a.activation(out=dtt[:, 3, :], in_=psa[:, 0:32], func=AF.Exp), reads=[pak], writes=["dtt"])
                        S.op('act', lambda a: a.activation(out=dtt[:, 4, :], in_=psa[:, 32:64], func=AF.Exp), reads=[pak], writes=["dtt"])
                        S.op('dve', lambda v: v.tensor_tensor(out=dtt[:, 6, :], in0=psa[:, 32:64], in1=dtt[:, 2, :], op=ALU.subtract),
                             reads=[pak, "dtt"], writes=["dtt"])
                        S.op('act', lambda a: a.activation(out=dtt[:, 6, :], in_=dtt[:, 6, :], func=AF.Exp), reads=["dtt"], writes=["dtt"])
                        S.op('dve', lambda v: v.tensor_tensor(out=dtt[:, 5, :], in0=dtt[:, 6, :], in1=dtt[:, 0, :], op=ALU.mult),
                             reads=["dtt"], writes=["dtt"])
                        S.op('dve', lambda v: v.tensor_tensor(out=xdt[:, :].rearrange("p (h q) -> p h q", q=64), in0=xs3,
                                                              in1=bc64(dtt[:, 0, :]), op=ALU.mult), reads=["x0", "dtt"], writes=["xdt"])
                        S.op('pool', lambda e: e.tensor_tensor(out=xw[:, :].rearrange("p (h q) -> p h q", q=64), in0=xs3,
                                                               in1=bc64(dtt[:, 5, :]), op=ALU.mult), reads=["x0", "dtt"], writes=["xw"])
                        for h in range(32):
                            if h % 3 == 0:
                                S.op('act', lambda a, h=h: a.activation(out=Rt[:, h, :], in_=Tri, func=AF.Copy, scale=dtt[:, 1, h:h + 1]),
                                     reads=["sm", "dtt"], writes=["Rt"])
                            elif h % 3 == 1:
                                S.op('dve', lambda v, h=h: v.tensor_scalar(out=Rt[:, h, :], in0=Tri, scalar1=dtt[:, 1, h:h + 1], scalar2=None,
                                                                           op0=ALU.mult), reads=["sm", "dtt"], writes=["Rt"])
                            else:
                                S.op('pool', lambda e, h=h: e.tensor_scalar(out=Rt[:, h, :], in0=Tri, scalar1=dtt[:, 1, h:h + 1], scalar2=None,
                                                                            op0=ALU.mult), reads=["sm", "dtt"], writes=["Rt"])
                        for g in range(4):
                            psY, pyk = next_ps()
                            for half in range(2):
                                hh = g * 8 + half * 4
                                sl = half
                                psd, pdk = next_ps()
                                S.op('pe', lambda t, psd=psd, hh=hh: t.matmul(
                                    psd[:, 0:512], lhsT=Mk, rhs=Rt[:, hh:hh + 4, :].rearrange("p a b -> p (a b)"), start=True, stop=True),
                                    reads=["sm", "Rt"], writes=[pdk])
                                S.op('act', lambda a, psd=psd, sl=sl: a.activation(out=Et[:, sl, :], in_=psd[:, 0:512], func=AF.Exp),
                                     reads=[pdk], writes=[f"Et{sl}"])
                                S.op('dve', lambda v, sl=sl, g=g: v.tensor_tensor(out=sct[:, sl, :], in0=Et[:, sl, :], in1=cbm[:, g, :], op=ALU.mult),
                                     reads=[f"Et{sl}", "cbm"], writes=[f"sct{sl}"])
                                for i in range(4):
                                    h = hh + i
                                    S.op('pe', lambda t, psY=psY, sl=sl, i=i, h=h, half=half: t.matmul(
                                        psY[:, (half * 4 + i) * 64:(half * 4 + i + 1) * 64], lhsT=sct[:, sl, i * 128:(i + 1) * 128],
                                        rhs=xdt[:, h * 64:(h + 1) * 64], start=True, stop=True),
                                        reads=[f"sct{sl}", "xdt"], writes=[pyk], signal=(i == 3))
                            psI, pik = next_ps()
                            S.op('pe', lambda t, psI=psI, g=g: t.matmul(psI[:, 0:512], lhsT=BCT[:, 4 + g, :], rhs=STb[:, g * 512:(g + 1) * 512],
                                                                        start=True, stop=True), reads=["BCT", "STb"], writes=[pik])
                            gs = slice(g * 512, (g + 1) * 512)
                            ts = g % 2
                            S.op('dve', lambda v, psI=psI, g=g, ts=ts: v.tensor_tensor(
                                out=tmpy[:, ts, :].rearrange("p (h q) -> p h q", q=64), in0=psI[:, 0:512].rearrange("p (h q) -> p h q", q=64),
                                in1=dtt[:, 3, g * 8:(g + 1) * 8].unsqueeze(2).to_broadcast([128, 8, 64]), op=ALU.mult),
                                reads=[pik, "dtt"], writes=[f"tmpy{ts}"])
                            if dd == 0:
                                S.op('dve', lambda v, psY=psY, ts=ts, gs=gs: v.tensor_tensor(out=yacc[:, gs], in0=tmpy[:, ts, :], in1=psY[:, 0:512], op=ALU.add),
                                     reads=[pyk, f"tmpy{ts}"], writes=["yacc"])
                            else:
                                S.op('pool', lambda e, ts=ts, gs=gs: e.tensor_tensor(out=yacc[:, gs], in0=yacc[:, gs], in1=tmpy[:, ts, :], op=ALU.add),
                                     reads=[f"tmpy{ts}", "yacc"], writes=["yacc"])
                                S.op('dve', lambda v, psY=psY, gs=gs: v.tensor_tensor(out=yacc[:, gs], in0=yacc[:, gs], in1=psY[:, 0:512], op=ALU.add),
                                     reads=[pyk, "yacc"], writes=["yacc"])
                            psS, psk = next_ps()
                            S.op('pe', lambda t, psS=psS, g=g, gs=gs: t.matmul(psS[:, 0:512], lhsT=xbf[:, 2048 + g * 128:2048 + (g + 1) * 128],
                                                                               rhs=xw[:, gs], start=True, stop=True), reads=["xbf", "xw"], writes=[psk])
                            S.op('pool', lambda e, g=g, gs=gs: e.tensor_tensor(
                                out=ST[:, gs].rearrange("p (h q) -> p h q", q=64), in0=ST[:, gs].rearrange("p (h q) -> p h q", q=64),
                                in1=dtt[:, 4, g * 8:(g + 1) * 8].unsqueeze(2).to_broadcast([128, 8, 64]), op=ALU.mult),
                                reads=["ST", "dtt"], writes=["ST"])
                            S.op('dve', lambda v, psS=psS, gs=gs: v.tensor_tensor(out=ST[:, gs], in0=ST[:, gs], in1=psS[:, 0:512], op=ALU.add),
                                 reads=[psk, "ST"], writes=["ST"])
                            S.op('act', lambda a, gs=gs: a.copy(out=STb[:, gs], in_=ST[:, gs]), reads=["ST"], writes=["STb"])
                        if dd == 0:
                            S.op('pool', lambda e: e.tensor_tensor(out=zt[:, :].rearrange("p (h q) -> p h q", q=64), in0=xs3,
                                                                   in1=bc64(dsk[:, :]), op=ALU.mult), reads=["x0", "dsk"], writes=["zt"])
                            S.op('dve', lambda v: v.tensor_tensor(out=yacc[:], in0=yacc[:], in1=zt[:], op=ALU.add), reads=["yacc", "zt"], writes=["yacc"])
                            S.dma('sp', y1_d[t0:t0 + 128, :], yacc[:], reads=["yacc"], writes=["y1_d"])
                        else:
                            S.op('act', lambda a: a.activation(out=zt[:], in_=zt[:], func=AF.Silu), reads=["zt"], writes=["zt"])
                            S.op('dve', lambda v: v.tensor_tensor(out=yacc[:], in0=yacc[:], in1=zt[:], op=ALU.mult), reads=["yacc", "zt"], writes=["yacc"])
                            S.op('dve', lambda v: v.scalar_tensor_tensor(out=ynb[:], in0=yacc[:], scalar=1.0, in1=yacc[:], op0=ALU.mult, op1=ALU.mult,
                                                                         accum_out=ssq[:, 0:1]), reads=["yacc"], writes=["ynb", "ssq"])
                            S.op('act', lambda a: a.activation(out=ssq[:, 1:2], in_=ssq[:, 0:1], func=AF.Sqrt, bias=epst[:, 0:1], scale=1.0 / 2048),
                                 reads=["ssq", "epst"], writes=["ssq"])
                            S.op('dve', lambda v: v.reciprocal(out=ssq[:, 2:3], in_=ssq[:, 1:2]), reads=["ssq"], writes=["ssq"])
                            S.op('dve', lambda v: v.scalar_tensor_tensor(out=ynb[:], in0=yacc[:], scalar=ssq[:, 2:3], in1=gn[:], op0=ALU.mult, op1=ALU.mult),
                                 reads=["yacc", "ssq", "gn"], writes=["ynb"])
                            for k0 in range(0, 16, 8):
                                pt, ptk = next_pst()
                                for jx in range(8):
                                    S.op('pe', lambda t, pt=pt, jx=jx, k0=k0: t.transpose(out=pt[:, jx * 128:(jx + 1) * 128],
                                                                                        in_=ynb[:, (k0 + jx) * 128:(k0 + jx + 1) * 128], identity=ident[:]),
                                         reads=["ynb", "ident"], writes=[ptk], signal=(jx == 7))
                                S.op('act', lambda a, pt=pt, k0=k0: a.copy(out=sTt[:, k0:k0 + 8, :], in_=pt[:, :].rearrange("p (j t) -> p j t", j=8)),
                                     reads=[ptk], writes=["sTt"])
                            S.dma('sp', s_d[:, :, t0:t0 + 128].rearrange("c p t -> p c t"), sTt[:], reads=["sTt"], writes=["s_d"])

            if "p3" in stages:
              with ExitStack() as es4:
                def sb4(name, shape, dt):
                    return es4.enter_context(nc.sbuf_tensor(f"s3_{name}_{layer}", shape, dt))
                xt = sb4("xt", [128, 4, D], F32)
                hT = sb4("hT", [128, 32, TT], BF16)
                hid = sb4("hid", [128, 20, TT], BF16)
                hb = sb4("hb", [128, D], BF16)
                gbc = sb4("gbc", [128, D], F32)
                ss = sb4("ss", [128, 8], F32)
                stg = sb4("stg", [128, 4, 512], F32)
                gat = sb4("gat", [128, 2, 2, TT], BF16)
                stg_rr = [0]

                def next_stg():
                    i = stg_rr[0] % 4
                    stg_rr[0] += 1
                    return i
                norm_to_hT, ffn = make_fns(xt, hT, hid, hb, gbc, ss, stg, next_stg, layer)
                aT = hid[:, 16:20, :]
                sT3 = hid[:, 0:16, :]
                wao = w["w_attn_out"][layer]
                wso = w["w_ssm_out"][layer]
                cnt = 0
                for tile in range(NTILE):
                    t0 = tile * TT
                    S.dma('sp', xt[:], y[t0:t0 + TT, :].rearrange("(b p) f -> p b f", p=128), reads=["ydram"], writes=["xt"])
                    S.dma('sp', aT, o_d[:, :, t0:t0 + TT].rearrange("c p t -> p c t"), reads=["o_d"], writes=["hid"])
                    S.dma('sp', sT3, s_d[:, :, t0:t0 + TT].rearrange("c p t -> p c t"), reads=["s_d"], writes=["hid"])
                    for b0 in range(0, D, 256):
                        wat, wak, _ = wload(wao, b0, 256)
                        wst, wsk, _ = wload(wso, b0, 256)
                        for ci in range(2):
                            fc = b0 // 128 + ci
                            gsl = cnt % 2
                            cnt += 1
                            S.dma('sp', gat[:, gsl, 0, :], g_d[fc, :, t0:t0 + TT], reads=["g_d"], writes=[f"gat{gsl}"])
                            S.dma('sp', gat[:, gsl, 1, :], g_d[32 + fc, :, t0:t0 + TT], reads=["g_d"], writes=[f"gat{gsl}"])
                            psa, pak = next_ps()
                            mm_group(psa[:, 0:TT], pak, 4, lambda kc, ci=ci, wat=wat: wat[:, kc, ci * 128:(ci + 1) * 128],
                                     lambda kc: aT[:, kc, :], [wak, "hid"])
                            pss, psk = next_ps()
                            mm_group(pss[:, 0:TT], psk, 16, lambda kc, ci=ci, wst=wst: wst[:, kc, ci * 128:(ci + 1) * 128],
                                     lambda kc: sT3[:, kc, :], [wsk, "hid"])
                            si = next_stg()
                            sj = next_stg()
                            S.op('dve', lambda v, psa=psa, si=si, gsl=gsl: v.tensor_tensor(out=stg[:, si, :], in0=psa[:, 0:TT], in1=gat[:, gsl, 0, :], op=ALU.mult),
                                 reads=[pak, f"gat{gsl}"], writes=[f"stg{si}"])
                            S.op('dve', lambda v, pss=pss, sj=sj, gsl=gsl: v.tensor_tensor(out=stg[:, sj, :], in0=pss[:, 0:TT], in1=gat[:, gsl, 1, :], op=ALU.mult),
                                 reads=[psk, f"gat{gsl}"], writes=[f"stg{sj}"])
                            S.op('dve', lambda e, si=si, sj=sj, fc=fc: e.tensor_tensor(out=hT[:, fc, :], in0=stg[:, si, :], in1=stg[:, sj, :], op=ALU.add),
                                 reads=[f"stg{si}", f"stg{sj}"], writes=["hT"])

                    def ev_out(ps, pskey, b, c0, wd):
                        S.op('dve', lambda v: v.tensor_tensor(out=xt[:, b, c0:c0 + wd], in0=xt[:, b, c0:c0 + wd], in1=ps[:, 0:wd], op=ALU.add),
                             reads=[pskey, "xt"], writes=["xt"])
                    gemm_tm(w["w_out"][layer], 0, D, hT, "hT", ev_out)
                    norm_to_hT("ffn2_norm")
                    ffn(w["ffn2_w_gate"][layer], w["ffn2_w_up"][layer], w["ffn2_w_down"][layer])
                    S.dma('sp', y[t0:t0 + TT, :].rearrange("(b p) f -> p b f", p=128), xt[:], reads=["xt"], writes=["ydram"])

        S.finish(["ydram"])
    return nc


def host_consts(conn):
    c = np.zeros((128, NCST), np.float32)
    for m in range(128):
        if m < 64:
            c[m + 64, C_RM + m] = -1.0
        else:
            c[m - 64, C_RM + m] = 1.0
    c[:, C_ONES:C_ONES + 128] = 1.0
    k = np.arange(128)[:, None]
    s = np.arange(128)[None, :]
    c[:, C_MF:C_MF + 128] = (k > s)
    c[:, C_TF:C_TF + 128] = (k <= s)
    c[:, C_MB:C_MB + 128] = (k < s)
    c[:, C_TB:C_TB + 128] = (k >= s)
    j = np.arange(256)[None, :]
    band = (np.abs(k + 64 - j) <= 64).astype(np.float32)
    lo = band.copy()
    lo[:, 0:64] *= conn
    hi = band.copy()
    hi[:, 192:256] *= conn
    c[:, C_AM:C_AM + 256] = band
    c[:, C_AM + 256:C_AM + 512] = lo
    c[:, C_AM + 512:C_AM + 768] = hi
    c[:, C_ID:C_ID + 128] = np.eye(128, dtype=np.float32)
    c[:, C_CV] = conn
    c[:, C_CV + 1] = 1.0
    c[127, C_CV + 1] = conn
    return c


def host_rope(seqlen):
    pos = (np.arange(T) % seqlen).astype(np.float32)
    inv = (10000.0 ** (-np.arange(0, 128, 2, dtype=np.float32) / 128)).astype(np.float32)
    ang = pos[None, :] * np.concatenate([inv, inv])[:, None]
    return np.cos(ang).astype(np.float32), np.sin(ang).astype(np.float32)


def tile_all(inputs, n_layers=L_FULL):
    tiled = {}
    for name in BIG_SEGS:
        W = np.asarray(inputs[name], dtype=np.float32)[:n_layers]
        for si, (st, en, bw) in enumerate(BIG_SEGS[name]):
            tiled[(name, si)] = tile_weight(W, st, en, bw)
    return tiled


def make_in_map(xc, inputs, conn, n_layers=L_FULL, tiled=None):
    if tiled is None:
        tiled = tile_all(inputs, n_layers)
    m = {"x": np.ascontiguousarray(xc, dtype=np.float32)}
    for name, shp in W_SPECS:
        if name in BIG_SEGS:
            for si, (st, en, bw) in enumerate(BIG_SEGS[name]):
                m[f"{name}_t{si}"] = tiled[(name, si)]
        else:
            m[name] = np.asarray(inputs[name], dtype=np.float32)[:n_layers]
    m["cst"] = host_consts(conn)
    cs, sn = host_rope(4096 if conn == 1.0 else 2048)
    m["cosT"], m["sinT"] = cs, sn
    return m


def kernel(**inputs):
    xp = np.asarray(inputs["x_prompt"], dtype=np.float32)
    xs = np.asarray(inputs["x_sample"], dtype=np.float32)
    nc = build_nc()
    tiled = tile_all(inputs)
    in_maps = []
    for c in range(8):
        if c < 4:
            in_maps.append(make_in_map(xp[c], inputs, 1.0, tiled=tiled))
        else:
            in_maps.append(make_in_map(xs[2 * (c - 4):2 * (c - 4) + 2].reshape(T, D), inputs, 0.0, tiled=tiled))
    res = run_bass_kernel_spmd(nc, in_maps, core_ids=list(range(8)))
    outs = [r["y"] for r in res.results]
    y_prompt = np.stack(outs[:4], axis=0)
    y_sample = np.concatenate(outs[4:], axis=0).reshape(8, 2048, D)
    return (y_prompt, y_sample)
```

```python
import math
from contextlib import ExitStack
import numpy as np
import concourse.bass as bass
import concourse.mybir as mybir
from concourse.bass_utils import run_bass_kernel_spmd

F32 = mybir.dt.float32
BF16 = mybir.dt.bfloat16
AF = mybir.ActivationFunctionType
ALU = mybir.AluOpType

D = 4096
T = 4096
TT = 512
NTILE = T // TT
L_FULL = 4
N_IN = 17984
EPS = 1e-6

W_SPECS = [
    ("ffn1_norm", [D]), ("ffn1_w_gate", [D, 2048]), ("ffn1_w_up", [D, 2048]),
    ("ffn1_w_down", [2048, D]), ("mix_norm", [D]), ("w_in", [D, N_IN]),
    ("q_norm", [128]), ("k_norm", [128]), ("conv_w", [3, 3072]), ("conv_b", [3072]),
    ("dt_bias", [2, 32]), ("a_log", [2, 32]), ("d_skip", [32]), ("ssm_norm", [2048]),
    ("w_attn_out", [512, D]), ("w_ssm_out", [2048, D]), ("w_out", [D, D]),
    ("ffn2_norm", [D]), ("ffn2_w_gate", [D, 2048]), ("ffn2_w_up", [D, 2048]),
    ("ffn2_w_down", [2048, D]),
]

BIG_SEGS = {
    "ffn1_w_gate": [(0, 2048, 256)], "ffn1_w_up": [(0, 2048, 256)], "ffn1_w_down": [(0, D, 256)],
    "w_in": [(0, 9728, 256), (9728, 9792, 64), (9792, N_IN, 256)],
    "w_attn_out": [(0, D, 256)], "w_ssm_out": [(0, D, 256)], "w_out": [(0, D, 256)],
    "ffn2_w_gate": [(0, 2048, 256)], "ffn2_w_up": [(0, 2048, 256)], "ffn2_w_down": [(0, D, 256)],
}


class TWL:
    def __init__(self, segs, layer, K):
        self.segs, self.layer, self.K = segs, layer, K

    def block(self, c0, width):
        for (st, en, bw, ap) in self.segs:
            if st <= c0 < en:
                assert (c0 - st) % bw == 0 and width == bw, (c0, width, st, bw)
                return ap[self.layer, (c0 - st) // bw]
        raise AssertionError(c0)


class TW:
    def __init__(self, segs, K):
        self.segs, self.K = segs, K

    def __getitem__(self, layer):
        return TWL(self.segs, layer, self.K)


def tile_weight(W, st, en, bw):
    Lw, K, _ = W.shape
    KC = K // 128
    NB = (en - st) // bw
    return np.ascontiguousarray(W[:, :, st:en].reshape(Lw, KC, 128, NB, bw).transpose(0, 3, 2, 1, 4))


C_RM, C_ONES, C_MF, C_TF, C_MB, C_TB, C_AM, C_CV = 0, 128, 256, 384, 512, 640, 768, 1536
C_ID = 1538
NCST = 1666


class Sched:
    P = 12000
    K = 16
    U = 900

    def __init__(self, nc, es):
        self.nc, self.es = nc, es
        self.eng = dict(pe=nc.tensor, act=nc.scalar, dve=nc.vector, pool=nc.gpsimd, sp=nc.sync)
        self.nsig = {e: 0 for e in self.eng}
        self.esems = {e: [] for e in self.eng}
        self.waited = {}
        self.lastw = {}
        self.rd_e = {}
        self.rd_d = {}
        self.ndma = {}
        self.dsems = {}
        self.nsem = 0

    def _newsem(self, name):
        self.nsem += 1
        return self.es.enter_context(self.nc.semaphore(name))

    def _esem(self, e, ep):
        while len(self.esems[e]) <= ep:
            self.esems[e].append(self._newsem(f"e_{e}_{len(self.esems[e])}"))
        return self.esems[e][ep]

    def _dsem(self, q, i):
        st = i // (self.K * self.U)
        lst = self.dsems.setdefault(q, [])
        while len(lst) <= st:
            lst.append([self._newsem(f"d_{q}_{len(lst)}_{j}") for j in range(self.K)])
        slot = i % self.K
        val = 16 * ((i % (self.K * self.U)) // self.K + 1)
        return lst[st][slot], st, slot, val

    def _wait(self, e, tok):
        if tok[0] == 'e':
            _, e2, n = tok
            if e2 == e and e == 'pe':
                return
            ep, val = (n - 1) // self.P, (n - 1) % self.P + 1
            if self.waited.get((e, e2, 'ep'), -1) > ep:
                return
            if self.waited.get((e, e2, ep), 0) >= val:
                return
            self.eng[e].wait_ge(self._esem(e2, ep), val)
            self.waited[(e, e2, ep)] = val
            self.waited[(e, e2, 'ep')] = max(ep, self.waited.get((e, e2, 'ep'), -1))
        else:
            _, q, i = tok
            sem, st, slot, val = self._dsem(q, i)
            key = (e, 'd', q, st, slot)
            if self.waited.get(key, 0) >= val:
                return
            self.eng[e].wait_ge(sem, val)
            self.waited[key] = val

    def _deps(self, reads, writes):
        toks = []
        for k in reads:
            t = self.lastw.get(k)
            if t is not None:
                toks.append(t)
        for k in writes:
            t = self.lastw.get(k)
            if t is not None:
                toks.append(t)
            for e2, n in self.rd_e.get(k, {}).items():
                toks.append(('e', e2, n))
            for t in self.rd_d.get(k, ()):
                toks.append(t)
        return toks

    def _register(self, tok, reads, writes):
        for k in reads:
            if tok[0] == 'e':
                d = self.rd_e.setdefault(k, {})
                d[tok[1]] = max(d.get(tok[1], 0), tok[2])
            else:
                self.rd_d.setdefault(k, []).append(tok)
        for k in writes:
            self.lastw[k] = tok
            self.rd_e[k] = {}
            self.rd_d[k] = []

    def op(self, e, fn, reads=(), writes=(), signal=True):
        for t in self._deps(reads, writes):
            self._wait(e, t)
        ins = fn(self.eng[e])
        if signal:
            self.nsig[e] += 1
            n = self.nsig[e]
            ins.then_inc(self._esem(e, (n - 1) // self.P), 1)
            tok = ('e', e, n)
        else:
            tok = ('e', e, self.nsig[e] + 1)
        self._register(tok, reads, writes)
        return ins

    def dma(self, q, out, in_, reads=(), writes=(), **kw):
        i = self.ndma.get(q, 0)
        toks = self._deps(reads, writes)
        if i >= self.K:
            toks.append(('d', q, i - self.K))
        for t in toks:
            self._wait(q, t)
        sem, st, slot, val = self._dsem(q, i)
        self.eng[q].dma_start(out=out, in_=in_, **kw).then_inc(sem, 16)
        self.ndma[q] = i + 1
        self._register(('d', q, i), reads, writes)

    def barrier(self):
        toks = []
        for e2 in self.eng:
            if self.nsig[e2] > 0:
                toks.append(('e', e2, self.nsig[e2]))
        for q, n in self.ndma.items():
            for i in range(max(0, n - self.K), n):
                toks.append(('d', q, i))
        for e in self.eng:
            for t in toks:
                self._wait(e, t)

    def finish(self, out_keys):
        for k in out_keys:
            t = self.lastw.get(k)
            if t is not None:
                self._wait('sp', t)
        for q, n in self.ndma.items():
            for i in range(max(0, n - self.K), n):
                self._wait('sp', ('d', q, i))


def build_nc(n_layers=L_FULL, dbg=False, stages=("p1", "att", "ssd", "p3")):
    nc = bass.Bass("TRN2", target_bir_lowering=False)
    x_in = nc.dram_tensor("x", [T, D], F32, kind="ExternalInput").ap()
    w = {}
    for name, shp in W_SPECS:
        if name in BIG_SEGS:
            K = shp[0]
            segs = []
            for si, (st, en, bw) in enumerate(BIG_SEGS[name]):
                ap = nc.dram_tensor(f"{name}_t{si}", [n_layers, (en - st) // bw, 128, K // 128, bw], F32, kind="ExternalInput").ap()
                segs.append((st, en, bw, ap))
            w[name] = TW(segs, K)
        else:
            w[name] = nc.dram_tensor(name, [n_layers] + shp, F32, kind="ExternalInput").ap()
    cst_d = nc.dram_tensor("cst", [128, NCST], F32, kind="ExternalInput").ap()
    cos_d = nc.dram_tensor("cosT", [128, T], F32, kind="ExternalInput").ap()
    sin_d = nc.dram_tensor("sinT", [128, T], F32, kind="ExternalInput").ap()
    y = nc.dram_tensor("y", [T, D], F32, kind="ExternalOutput").ap()
    sk = "ExternalOutput" if dbg else "Internal"
    qk_d = nc.dram_tensor("qk_s", [24, 128, T], BF16, kind=sk).ap()
    v_d = nc.dram_tensor("v_s", [T, 1536], BF16, kind=sk).ap()
    z_d = nc.dram_tensor("z_s", [T, 2048], F32, kind=sk).ap()
    xbc_d = nc.dram_tensor("xbc_s", [T, 3072], F32, kind=sk).ap()
    dtr_d = nc.dram_tensor("dtr_s", [T, 64], F32, kind=sk).ap()
    g_d = nc.dram_tensor("g_s", [64, 128, T], BF16, kind=sk).ap()
    o_d = nc.dram_tensor("o_s", [4, 128, T], BF16, kind=sk).ap()
    y1_d = nc.dram_tensor("y1_s", [T, 2048], F32, kind=sk).ap()
    s_d = nc.dram_tensor("s_s", [16, 128, T], BF16, kind=sk).ap()
    xc_d = nc.dram_tensor("xc_s", [T, 3072], F32, kind="Internal").ap()

    es = ExitStack()
    with es:
        S = Sched(nc, es)

        def sb(name, shape, dt):
            return es.enter_context(nc.sbuf_tensor("sb_" + name, shape, dt))

        psb = [es.enter_context(nc.psum_tensor(f"ps{i}", [128, 512], F32)) for i in range(6)]
        pst = [es.enter_context(nc.psum_tensor(f"pst{i}", [128, 1024], BF16)) for i in range(2)]
        ps_rr = [0]
        pst_rr = [0]

        def next_ps():
            i = ps_rr[0] % 6
            ps_rr[0] += 1
            return psb[i], f"ps{i}"

        def next_pst():
            i = pst_rr[0] % 2
            pst_rr[0] += 1
            return pst[i], f"pst{i}"

        cst = sb("cst", [128, 256], F32)
        S.dma('sp', cst[:], cst_d[:, 0:256], writes=["cst"])
        idf = sb("idf", [128, 128], F32)
        S.dma('sp', idf[:], cst_d[:, C_ID:C_ID + 128], writes=["idf"])
        ones_bf = sb("ones_bf", [128, 128], BF16)
        S.op('dve', lambda v: v.tensor_copy(out=ones_bf[:], in_=cst[:, C_ONES:C_ONES + 128]), reads=["cst"], writes=["ones_bf"])
        ident = sb("ident", [128, 128], BF16)
        S.op('dve', lambda v: v.tensor_copy(out=ident[:], in_=idf[:]), reads=["idf"], writes=["ident"])
        epst = sb("epst", [128, 1], F32)
        S.op('dve', lambda v: v.memset(epst[:], EPS), writes=["epst"])
        Rm = cst[:, C_RM:C_RM + 128]
        ones_f = cst[:, C_ONES:C_ONES + 128]

        NW = 3
        wbuf = sb("wbuf", [128, NW, 32, 256], BF16)
        w_rr = [0]

        def wload(wap, c0, width):
            K = wap.K
            KC = K // 128
            s = w_rr[0] % NW
            w_rr[0] += 1
            key = f"w{s}"
            S.dma('pool', wbuf[:, s, 0:KC, 0:width], wap.block(c0, width), writes=[key])
            return wbuf[:, s], key, KC

        def mm_group(ps_ap, pskey, KC, lhs_fn, rhs_fn, rkeys):
            for kc in range(KC):
                S.op('pe', (lambda t, kc=kc: t.matmul(ps_ap, lhsT=lhs_fn(kc), rhs=rhs_fn(kc),
                                                      start=(kc == 0), stop=(kc == KC - 1))),
                     reads=rkeys, writes=[pskey], signal=(kc == KC - 1))

        def gemm_fm(wap, c0, ncols, hT, hkey, evac):
            for b0 in range(c0, c0 + ncols, 256):
                wd = min(256, c0 + ncols - b0)
                wt, wkey, KC = wload(wap, b0, wd)
                for ci in range(wd // 128):
                    ps, pskey = next_ps()
                    mm_group(ps[:, 0:TT], pskey, KC,
                             lambda kc, ci=ci, wt=wt: wt[:, kc, ci * 128:(ci + 1) * 128],
                             lambda kc: hT[:, kc, 0:TT], [wkey, hkey])
                    evac(ps, pskey, b0 + ci * 128)

        def gemm_tm(wap, c0, ncols, hT, hkey, evac, blk=256):
            for b0 in range(c0, c0 + ncols, blk):
                wd = min(blk, c0 + ncols - b0)
                wt, wkey, KC = wload(wap, b0, wd)
                for b in range(TT // 128):
                    ps, pskey = next_ps()
                    mm_group(ps[:, 0:wd], pskey, KC,
                             lambda kc, b=b: hT[:, kc, b * 128:(b + 1) * 128],
                             lambda kc, wt=wt, wd=wd: wt[:, kc, 0:wd], [wkey, hkey])
                    evac(ps, pskey, b, b0, wd)

        def make_fns(xt, hT, hid, hb, gbc, ss, stg, next_stg, layer):
            def norm_to_hT(gname):
                S.dma('sp', gbc[:], w[gname][layer].partition_broadcast(128), writes=["gbc"])
                for b in range(4):
                    S.op('dve', lambda v, b=b: v.scalar_tensor_tensor(
                        out=hb[:], in0=xt[:, b, :], scalar=1.0, in1=xt[:, b, :],
                        op0=ALU.mult, op1=ALU.mult, accum_out=ss[:, 0:1]),
                        reads=["xt"], writes=["hb", "ss"])
                    S.op('act', lambda a: a.activation(out=ss[:, 1:2], in_=ss[:, 0:1], func=AF.Sqrt,
                                                       bias=epst[:, 0:1], scale=1.0 / D),
                         reads=["ss", "epst"], writes=["ss"])
                    S.op('dve', lambda v: v.reciprocal(out=ss[:, 2:3], in_=ss[:, 1:2]), reads=["ss"], writes=["ss"])
                    S.op('dve', lambda v, b=b: v.scalar_tensor_tensor(
                        out=hb[:], in0=xt[:, b, :], scalar=ss[:, 2:3], in1=gbc[:],
                        op0=ALU.mult, op1=ALU.mult), reads=["xt", "ss", "gbc"], writes=["hb"])
                    for k0 in range(0, 32, 8):
                        pt, ptkey = next_pst()
                        for j in range(8):
                            kc = k0 + j
                            S.op('pe', lambda t, kc=kc, j=j, pt=pt: t.transpose(
                                out=pt[:, j * 128:(j + 1) * 128], in_=hb[:, kc * 128:(kc + 1) * 128], identity=ident[:]),
                                reads=["hb", "ident"], writes=[ptkey], signal=(j == 7))
                        eng = 'act' if (k0 // 8) % 2 == 0 else 'pool'
                        if eng == 'act':
                            S.op('act', lambda a, k0=k0, b=b, pt=pt: a.copy(
                                out=hT[:, k0:k0 + 8, b * 128:(b + 1) * 128],
                                in_=pt[:, :].rearrange("p (j t) -> p j t", j=8)),
                                reads=[ptkey], writes=["hT"])
                        else:
                            S.op('dve', lambda v, k0=k0, b=b, pt=pt: v.tensor_copy(
                                out=hT[:, k0:k0 + 8, b * 128:(b + 1) * 128],
                                in_=pt[:, :].rearrange("p (j t) -> p j t", j=8)),
                                reads=[ptkey], writes=["hT"])

            def ffn(wg, wu, wdn):
                for nb in range(0, 2048, 128):
                    pass
                for b0 in range(0, 2048, 256):
                    wgt, wgk, KC = wload(wg, b0, 256)
                    wut, wuk, _ = wload(wu, b0, 256)
                    for ci in range(2):
                        psg, pgk = next_ps()
                        mm_group(psg[:, 0:TT], pgk, KC,
                                 lambda kc, ci=ci, wgt=wgt: wgt[:, kc, ci * 128:(ci + 1) * 128],
                                 lambda kc: hT[:, kc, 0:TT], [wgk, "hT"])
                        psu, puk = next_ps()
                        mm_group(psu[:, 0:TT], puk, KC,
                                 lambda kc, ci=ci, wut=wut: wut[:, kc, ci * 128:(ci + 1) * 128],
                                 lambda kc: hT[:, kc, 0:TT], [wuk, "hT"])
                        si = next_stg()
                        S.op('act', lambda a, psg=psg, si=si: a.activation(out=stg[:, si, :], in_=psg[:, 0:TT], func=AF.Silu),
                             reads=[pgk], writes=[f"stg{si}"])
                        hc = b0 // 128 + ci
                        S.op('dve', lambda v, psu=psu, si=si, hc=hc: v.tensor_tensor(
                            out=hid[:, hc, :], in0=stg[:, si, :], in1=psu[:, 0:TT], op=ALU.mult),
                            reads=[puk, f"stg{si}"], writes=["hid"])

                def ev_down(ps, pskey, b, c0, wd):
                    S.op('dve', lambda v: v.scalar_tensor_tensor(
                        out=xt[:, b, c0:c0 + wd], in0=ps[:, 0:wd], scalar=0.5, in1=xt[:, b, c0:c0 + wd],
                        op0=ALU.mult, op1=ALU.add), reads=[pskey, "xt"], writes=["xt"])
                gemm_tm(wdn, 0, D, hid, "hid", ev_down)


            return norm_to_hT, ffn

        for layer in range(n_layers):
            x_src = x_in if layer == 0 else y
            if "p1" in stages:
              with ExitStack() as es1:
                S.barrier()
                def sb1(name, shape, dt):
                    return es1.enter_context(nc.sbuf_tensor(f"sb_{name}_{layer}", shape, dt))
                xt = sb1("xt", [128, 4, D], F32)
                hT = sb1("hT", [128, 32, TT], BF16)
                hid = sb1("hid", [128, 16, TT], BF16)
                hb = sb1("hb", [128, D], BF16)
                gbc = sb1("gbc", [128, D], F32)
                ss = sb1("ss", [128, 8], F32)
                cosS = sb1("cosS", [128, TT], F32)
                sinS = sb1("sinS", [128, TT], F32)
                stg = sb1("stg", [128, 4, 512], F32)
                stgb = sb1("stgb", [128, 4, 512], BF16)
                qkn = sb1("qkn", [128, 2], F32)
                dtst = sb1("dtst", [128, 4, 64], F32)
                stg_rr = [0]

                def next_stg():
                    i = stg_rr[0] % 4
                    stg_rr[0] += 1
                    return i

                S.dma('sp', qkn[:, 0:1], w["q_norm"][layer].rearrange("(p o) -> p o", o=1), writes=["qkn"])
                S.dma('sp', qkn[:, 1:2], w["k_norm"][layer].rearrange("(p o) -> p o", o=1), writes=["qkn"])

                norm_to_hT, ffn = make_fns(xt, hT, hid, hb, gbc, ss, stg, next_stg, layer)

                for tile in range(NTILE):
                    t0 = tile * TT
                    S.dma('sp', xt[:], x_src[t0:t0 + TT, :].rearrange("(b p) f -> p b f", p=128),
                          reads=["ydram"], writes=["xt"])
                    S.dma('sp', cosS[:], cos_d[:, t0:t0 + TT], writes=["cosS"])
                    S.dma('sp', sinS[:], sin_d[:, t0:t0 + TT], writes=["sinS"])
                    norm_to_hT("ffn1_norm")
                    ffn(w["ffn1_w_gate"][layer], w["ffn1_w_up"][layer], w["ffn1_w_down"][layer])
                    norm_to_hT("mix_norm")
                    win = w["w_in"][layer]

                    def ev_qk(ps, pskey, col):
                        hd = col // 128
                        isk = 1 if hd >= 12 else 0
                        si = next_stg()
                        sj = next_stg()
                        S.op('act', lambda a: a.copy(out=stg[:, si, :], in_=ps[:, 0:TT]), reads=[pskey], writes=[f"stg{si}"])
                        S.op('act', lambda a: a.activation(out=stg[:, sj, :], in_=ps[:, 0:TT], func=AF.Square),
                             reads=[pskey], writes=[f"stg{sj}"])
                        p2, p2k = next_ps()
                        S.op('pe', lambda t: t.matmul(p2[:, 0:TT], lhsT=ones_f, rhs=stg[:, sj, :], start=True, stop=True),
                             reads=["cst", f"stg{sj}"], writes=[p2k])
                        S.op('act', lambda a: a.activation(out=stg[:, sj, :], in_=p2[:, 0:TT], func=AF.Sqrt,
                                                           bias=epst[:, 0:1], scale=1.0 / 128),
                             reads=[p2k, "epst"], writes=[f"stg{sj}"])
                        S.op('dve', lambda v: v.reciprocal(out=stg[:, sj, :], in_=stg[:, sj, :]),
                             reads=[f"stg{sj}"], writes=[f"stg{sj}"])
                        S.op('dve', lambda v: v.scalar_tensor_tensor(
                            out=stg[:, si, :], in0=stg[:, si, :], scalar=qkn[:, isk:isk + 1], in1=stg[:, sj, :],
                            op0=ALU.mult, op1=ALU.mult), reads=[f"stg{si}", f"stg{sj}", "qkn"], writes=[f"stg{si}"])
                        p3, p3k = next_ps()
                        S.op('pe', lambda t: t.matmul(p3[:, 0:TT], lhsT=Rm, rhs=stg[:, si, :], start=True, stop=True),
                             reads=["cst", f"stg{si}"], writes=[p3k])
                        S.op('dve', lambda v: v.tensor_tensor(out=stg[:, sj, :], in0=p3[:, 0:TT], in1=sinS[:], op=ALU.mult),
                             reads=[p3k, "sinS"], writes=[f"stg{sj}"])
                        S.op('dve', lambda g: g.tensor_tensor(out=stg[:, si, :], in0=stg[:, si, :], in1=cosS[:], op=ALU.mult),
                             reads=[f"stg{si}", "cosS"], writes=[f"stg{si}"])
                        S.op('dve', lambda v: v.tensor_tensor(out=stgb[:, si, :], in0=stg[:, si, :], in1=stg[:, sj, :], op=ALU.add),
                             reads=[f"stg{si}", f"stg{sj}"], writes=[f"stgb{si}"])
                        S.dma('sp', qk_d[hd, :, t0:t0 + TT], stgb[:, si, :], reads=[f"stgb{si}"], writes=["qk_d"])
                    gemm_fm(win, 0, 3072, hT, "hT", ev_qk)

                    def ev_v(ps, pskey, b, c0, wd):
                        si = next_stg()
                        S.op('act', lambda a: a.copy(out=stgb[:, si, 0:wd], in_=ps[:, 0:wd]), reads=[pskey], writes=[f"stgb{si}"])
                        S.dma('sp', v_d[t0 + b * 128:t0 + (b + 1) * 128, c0 - 3072:c0 - 3072 + wd], stgb[:, si, 0:wd],
                              reads=[f"stgb{si}"], writes=["v_d"])
                    gemm_tm(win, 3072, 1536, hT, "hT", ev_v)

                    def ev_zx(ps, pskey, b, c0, wd):
                        si = next_stg()
                        if si % 2 == 0:
                            S.op('act', lambda a: a.copy(out=stg[:, si, 0:wd], in_=ps[:, 0:wd]), reads=[pskey], writes=[f"stg{si}"])
                        else:
                            S.op('dve', lambda v: v.tensor_copy(out=stg[:, si, 0:wd], in_=ps[:, 0:wd]), reads=[pskey], writes=[f"stg{si}"])
                        if c0 < 6656:
                            dst = z_d[t0 + b * 128:t0 + (b + 1) * 128, c0 - 4608:c0 - 4608 + wd]
                            dk = "z_d"
                        else:
                            dst = xbc_d[t0 + b * 128:t0 + (b + 1) * 128, c0 - 6656:c0 - 6656 + wd]
                            dk = "xbc_d"
                        S.dma('sp', dst, stg[:, si, 0:wd], reads=[f"stg{si}"], writes=[dk])
                    gemm_tm(win, 4608, 2048 + 3072, hT, "hT", ev_zx)

                    def ev_dt(ps, pskey, b, c0, wd):
                        S.op('act', lambda a: a.copy(out=dtst[:, b, :], in_=ps[:, 0:64]), reads=[pskey], writes=[f"dtst{b}"])
                        S.dma('sp', dtr_d[t0 + b * 128:t0 + (b + 1) * 128, :], dtst[:, b, :], reads=[f"dtst{b}"], writes=["dtr_d"])
                    gemm_tm(win, 9728, 64, hT, "hT", ev_dt)

                    def ev_g(ps, pskey, col):
                        si = next_stg()
                        S.op('act', lambda a: a.activation(out=stgb[:, si, :], in_=ps[:, 0:TT], func=AF.Sigmoid),
                             reads=[pskey], writes=[f"stgb{si}"])
                        S.dma('sp', g_d[(col - 9792) // 128, :, t0:t0 + TT], stgb[:, si, :], reads=[f"stgb{si}"], writes=["g_d"])
                    gemm_fm(win, 9792, 8192, hT, "hT", ev_g)

                    S.dma('sp', y[t0:t0 + TT, :].rearrange("(b p) f -> p b f", p=128), xt[:], reads=["xt"], writes=["ydram"])

            if "att" in stages:
              with ExitStack() as es2:
                S.barrier()
                def sb2(name, shape, dt):
                    return es2.enter_context(nc.sbuf_tensor(f"sa_{name}_{layer}", shape, dt))
                QS = [sb2(f"QS{g}", [128, T], BF16) for g in range(3)]
                KS = [sb2(f"KS{g}", [128, T], BF16) for g in range(3)]
                VS = [sb2(f"VS{g}", [128, 32, 128], BF16) for g in range(3)]
                accn = sb2("accn", [128, T], F32)
                accd = sb2("accd", [128, T], F32)
                obf = sb2("obf", [128, T], BF16)
                tmpq = sb2("tmpq", [128, T], BF16)
                am = sb2("am", [128, 768], F32)
                pex = sb2("pex", [128, 2, 256], F32)
                pbf = sb2("pbf", [128, 2, 256], BF16)
                S.dma('sp', am[:], cst_d[:, C_AM:C_AM + 768], writes=["am"])
                unit = 0
                for j in range(4):
                    for g, d in enumerate((1, 4, 16)):
                        n = T // d
                        nch = n // 128
                        for (dst, dkey, hidx) in ((QS[g], f"QS{g}", g * 4 + j), (KS[g], f"KS{g}", 12 + g * 4 + j)):
                            if d == 1:
                                S.dma('sp', dst[:], qk_d[hidx], reads=["qk_d"], writes=[dkey])
                            else:
                                S.dma('sp', tmpq[:], qk_d[hidx], reads=["qk_d"], writes=["tmpq"])
                                S.op('pool', lambda e, dst=dst, d=d: e.tensor_copy(
                                    out=dst[:, :].rearrange("p (dd m) -> p dd m", dd=d),
                                    in_=tmpq[:, :].rearrange("p (m dd) -> p dd m", dd=d)),
                                    reads=["tmpq"], writes=[dkey])
                        vsrc = v_d[:, (g * 4 + j) * 128:(g * 4 + j + 1) * 128].rearrange("(m dd) f -> dd m f", dd=d)
                        for r in range(d):
                            S.dma('sp', VS[g][:, r * nch:(r + 1) * nch, :],
                                  vsrc[r].rearrange("(c i) f -> i c f", i=128), reads=["v_d"], writes=[f"VS{g}"])
                    S.op('pool', lambda e: e.memset(accn[:], 0.0), writes=["accn"])
                    S.op('pool', lambda e: e.memset(accd[:], 0.0), writes=["accd"])
                    for g, d in enumerate((1, 4, 16)):
                        n = T // d
                        nch = n // 128
                        cb = nch // 2
                        Qv = QS[g][:, :].rearrange("p (dd m) -> p dd m", dd=d)
                        Kv = KS[g][:, :].rearrange("p (dd m) -> p dd m", dd=d)
                        An = accn[:, :].rearrange("p (m dd) -> p dd m", dd=d)
                        Ad = accd[:, :].rearrange("p (m dd) -> p dd m", dd=d)
                        for r in range(d):
                            for c in range(nch):
                                qlo = max(0, 128 * c - 64)
                                qhi = min(n, 128 * c + 192)
                                nq = qhi - qlo
                                j0 = qlo - (128 * c - 64)
                                mi = 1 if c == cb else (2 if c == cb - 1 else 0)
                                mask = am[:, mi * 256 + j0:mi * 256 + j0 + nq]
                                sl = unit % 2
                                unit += 1
                                ps, pk = next_ps()
                                S.op('pe', lambda t, ps=ps, c=c, r=r, qlo=qlo, qhi=qhi, nq=nq, Kv=Kv, Qv=Qv: t.matmul(
                                    ps[:, 0:nq], lhsT=Kv[:, r, 128 * c:128 * c + 128], rhs=Qv[:, r, qlo:qhi],
                                    start=True, stop=True), reads=[f"KS{g}", f"QS{g}"], writes=[pk])
                                S.op('act', lambda a, ps=ps, sl=sl, nq=nq: a.activation(
                                    out=pex[:, sl, 0:nq], in_=ps[:, 0:nq], func=AF.Exp, scale=128.0 ** -0.5),
                                    reads=[pk], writes=[f"pex{sl}"])
                                S.op('pool', lambda e, sl=sl, nq=nq, mask=mask: e.tensor_tensor(
                                    out=pbf[:, sl, 0:nq], in0=pex[:, sl, 0:nq], in1=mask, op=ALU.mult),
                                    reads=[f"pex{sl}", "am"], writes=[f"pbf{sl}"])
                                psn, pnk = next_ps()
                                psd, pdk = next_ps()
                                bi = r * nch + c
                                S.op('pe', lambda t, psn=psn, bi=bi, sl=sl, nq=nq, g=g: t.matmul(
                                    psn[:, 0:nq], lhsT=VS[g][:, bi, :], rhs=pbf[:, sl, 0:nq], start=True, stop=True),
                                    reads=[f"VS{g}", f"pbf{sl}"], writes=[pnk])
                                S.op('pe', lambda t, psd=psd, sl=sl, nq=nq: t.matmul(
                                    psd[:, 0:nq], lhsT=ones_bf[:], rhs=pbf[:, sl, 0:nq], start=True, stop=True),
                                    reads=["ones_bf", f"pbf{sl}"], writes=[pdk])
                                S.op('dve', lambda v, psn=psn, r=r, qlo=qlo, qhi=qhi, nq=nq, An=An: v.tensor_tensor(
                                    out=An[:, r, qlo:qhi], in0=An[:, r, qlo:qhi], in1=psn[:, 0:nq], op=ALU.add),
                                    reads=[pnk, "accn"], writes=["accn"])
                                S.op('dve', lambda v, psd=psd, r=r, qlo=qlo, qhi=qhi, nq=nq, Ad=Ad: v.tensor_tensor(
                                    out=Ad[:, r, qlo:qhi], in0=Ad[:, r, qlo:qhi], in1=psd[:, 0:nq], op=ALU.add),
                                    reads=[pdk, "accd"], writes=["accd"])
                    S.op('dve', lambda v: v.reciprocal(out=accd[:], in_=accd[:]), reads=["accd"], writes=["accd"])
                    S.op('dve', lambda v: v.tensor_tensor(out=obf[:], in0=accn[:], in1=accd[:], op=ALU.mult),
                         reads=["accn", "accd"], writes=["obf"])
                    S.dma('sp', o_d[j], obf[:], reads=["obf"], writes=["o_d"])

            if "ssd" in stages:
              with ExitStack() as es3:
                S.barrier()
                def sb3(name, shape, dt):
                    return es3.enter_context(nc.sbuf_tensor(f"ss_{name}_{layer}", shape, dt))
                sm = sb3("sm", [128, 512], F32)
                S.dma('sp', sm[:], cst_d[:, C_MF:C_MF + 512], writes=["sm"])
                cv = sb3("cv", [128, 2], F32)
                S.dma('sp', cv[:], cst_d[:, C_CV:C_CV + 2], writes=["cv"])
                cw = wbuf[:, :, :, :].rearrange("p a b c -> p (a b c)").bitcast(F32).rearrange("p (a c) -> p a c", a=4)
                WK = ["w0", "w1", "w2"]
                S.dma('sp', cw[:, 0:3, :], w["conv_w"][layer].partition_broadcast(128), writes=WK)
                S.dma('sp', cw[:, 3, :], w["conv_b"][layer].partition_broadcast(128), writes=WK)
                dtb = sb3("dtb", [128, 64], F32)
                Abc = sb3("Abc", [128, 64], F32)
                dsk = sb3("dsk", [128, 32], F32)
                gn = sb3("gn", [128, 2048], F32)
                S.dma('sp', dtb[:], w["dt_bias"][layer].rearrange("a b -> (a b)").partition_broadcast(128), writes=["dtb"])
                S.dma('sp', Abc[:], w["a_log"][layer].rearrange("a b -> (a b)").partition_broadcast(128), writes=["Abc"])
                S.dma('sp', dsk[:], w["d_skip"][layer].partition_broadcast(128), writes=["dsk"])
                S.dma('sp', gn[:], w["ssm_norm"][layer].partition_broadcast(128), writes=["gn"])
                S.op('act', lambda a: a.activation(out=Abc[:], in_=Abc[:], func=AF.Exp), reads=["Abc"], writes=["Abc"])
                S.op('dve', lambda v: v.tensor_scalar(out=Abc[:], in0=Abc[:], scalar1=-1.0, scalar2=None, op0=ALU.mult),
                     reads=["Abc"], writes=["Abc"])
                tri4 = sb3("tri4", [128, 2, 4, 128], F32)
                for dd in range(2):
                    for rep in range(4):
                        S.op('dve', lambda v, dd=dd, rep=rep: v.tensor_copy(
                            out=tri4[:, dd, rep, :], in_=sm[:, 128 + 256 * dd:256 + 256 * dd]), reads=["sm"], writes=["tri4"])
                xm = sb3("xm", [128, 3072], F32)
                x0 = sb3("x0", [128, 3072], F32)
                xp = sb3("xp", [128, 3072], F32)
                xbf = sb3("xbf", [128, 3072], BF16)
                BCT = sb3("BCT", [128, 8, 128], BF16)
                cbm = sb3("cbm", [128, 4, 512], F32)
                Rt = sb3("Rt", [128, 32, 128], F32)
                Et = sb3("Et", [128, 2, 512], F32)
                sct = sb3("sct", [128, 2, 512], BF16)
                xdt = sb3("xdt", [128, 2048], BF16)
                xw = sb3("xw", [128, 2048], BF16)
                yacc = sb3("yacc", [128, 2048], F32)
                tmpy = sb3("tmpy", [128, 2, 512], F32)
                ST = sb3("ST", [128, 2048], F32)
                STb = sb3("STb", [128, 2048], BF16)
                zt = sb3("zt", [128, 2048], F32)
                ynb = sb3("ynb", [128, 2048], BF16)
                sTt = sb3("sTt", [128, 16, 128], BF16)
                dtt = sb3("dtt", [128, 8, 32], F32)
                dtr = sb3("dtr", [128, 64], F32)
                ssq = sb3("ssq", [128, 4], F32)
                xs3 = x0[:, 0:2048].rearrange("p (h q) -> p h q", q=64)

                def bc64(ap32):
                    return ap32.unsqueeze(2).to_broadcast([128, 32, 64])

                for dd in range(2):
                    Mk = sm[:, 256 * dd:256 * dd + 128]
                    Tri = sm[:, 256 * dd + 128:256 * dd + 256]
                    S.op('pool', lambda e: e.memset(ST[:], 0.0), writes=["ST"])
                    S.op('pool', lambda e: e.memset(STb[:], 0.0), writes=["STb"])
                    order = list(range(32)) if dd == 0 else list(range(31, -1, -1))
                    for c in order:
                        t0 = c * 128
                        if dd == 0:
                            S.dma('sp', x0[:], xbc_d[t0:t0 + 128, :], reads=["xbc_d"], writes=["x0"])
                            if c == 0:
                                S.op('pool', lambda e: e.memset(xm[0:1, :], 0.0), writes=["xm"])
                                S.dma('sp', xm[1:128, :], xbc_d[0:127, :], reads=["xbc_d"], writes=["xm"])
                            else:
                                S.dma('sp', xm[:], xbc_d[t0 - 1:t0 + 127, :], reads=["xbc_d"], writes=["xm"])
                            if c == 31:
                                S.op('pool', lambda e: e.memset(xp[:], 0.0), writes=["xp"])
                                S.dma('sp', xp[0:127, :], xbc_d[t0 + 1:t0 + 128, :], reads=["xbc_d"], writes=["xp"])
                            else:
                                S.dma('sp', xp[:], xbc_d[t0 + 1:t0 + 129, :], reads=["xbc_d"], writes=["xp"])
                        else:
                            S.dma('sp', x0[:], xc_d[t0:t0 + 128, :], reads=["xc_d"], writes=["x0"])
                        S.dma('sp', dtr[:], dtr_d[t0:t0 + 128, :], reads=["dtr_d"], writes=["dtr"])
                        if dd == 1:
                            S.dma('sp', zt[:], z_d[t0:t0 + 128, :], reads=["z_d"], writes=["zt"])
                            S.dma('sp', yacc[:], y1_d[t0:t0 + 128, :], reads=["y1_d"], writes=["yacc"])
                        if dd == 0:
                            if c == 16:
                                S.op('dve', lambda v: v.tensor_scalar(out=xm[0:1, :], in0=xm[0:1, :], scalar1=cv[0:1, 0:1],
                                                                      scalar2=None, op0=ALU.mult), reads=["xm", "cv"], writes=["xm"])
                            if c == 15:
                                S.op('dve', lambda v: v.tensor_scalar(out=xp[:], in0=xp[:], scalar1=cv[:, 1:2],
                                                                      scalar2=None, op0=ALU.mult), reads=["xp", "cv"], writes=["xp"])
                        if (dd == 0 and c == 16) or (dd == 1 and c == 15):
                            S.op('dve', lambda v: v.tensor_scalar(out=ST[:], in0=ST[:], scalar1=cv[:, 0:1], scalar2=None,
                                                                  op0=ALU.mult), reads=["ST", "cv"], writes=["ST"])
                            S.op('act', lambda a: a.copy(out=STb[:], in_=ST[:]), reads=["ST"], writes=["STb"])
                        if dd == 0:
                            S.op('pool', lambda e: e.tensor_tensor(out=xm[:], in0=xm[:], in1=cw[:, 0, :], op=ALU.mult), reads=["xm"] + WK, writes=["xm"])
                            S.op('dve', lambda v: v.tensor_tensor(out=x0[:], in0=x0[:], in1=cw[:, 1, :], op=ALU.mult), reads=["x0"] + WK, writes=["x0"])
                            S.op('pool', lambda e: e.tensor_tensor(out=xp[:], in0=xp[:], in1=cw[:, 2, :], op=ALU.mult), reads=["xp"] + WK, writes=["xp"])
                            S.op('dve', lambda v: v.tensor_tensor(out=x0[:], in0=x0[:], in1=xm[:], op=ALU.add), reads=["x0", "xm"], writes=["x0"])
                            S.op('pool', lambda e: e.tensor_tensor(out=xp[:], in0=xp[:], in1=cw[:, 3, :], op=ALU.add), reads=["xp"] + WK, writes=["xp"])
                            S.op('dve', lambda v: v.tensor_tensor(out=x0[:], in0=x0[:], in1=xp[:], op=ALU.add), reads=["x0", "xp"], writes=["x0"])
                            S.op('act', lambda a: a.activation(out=x0[:], in_=x0[:], func=AF.Silu), reads=["x0"], writes=["x0"])
                            S.dma('sp', xc_d[t0:t0 + 128, :], x0[:], reads=["x0"], writes=["xc_d"])
                        S.op('act', lambda a: a.copy(out=xbf[:], in_=x0[:]), reads=["x0"], writes=["xbf"])
                        pt, ptk = next_pst()
                        for i in range(8):
                            S.op('pe', lambda t, i=i, pt=pt: t.transpose(out=pt[:, i * 128:(i + 1) * 128],
                                                                         in_=xbf[:, 2048 + i * 128:2048 + (i + 1) * 128], identity=ident[:]),
                                 reads=["xbf", "ident"], writes=[ptk], signal=(i == 7))
                        S.op('act', lambda a, pt=pt: a.copy(out=BCT[:, :, :], in_=pt[:, :].rearrange("p (j t) -> p j t", j=8)),
                             reads=[ptk], writes=["BCT"])
                        for g in range(4):
                            ps, pk = next_ps()
                            for rep in range(4):
                                S.op('pe', lambda t, ps=ps, g=g, rep=rep: t.matmul(ps[:, rep * 128:(rep + 1) * 128], lhsT=BCT[:, g, :],
                                                                                   rhs=BCT[:, 4 + g, :], start=True, stop=True),
                                     reads=["BCT"], writes=[pk], signal=(rep == 3))
                            S.op('dve', lambda v, ps=ps, g=g: v.tensor_tensor(
                                out=cbm[:, g, :], in0=ps[:, 0:512], in1=tri4[:, dd, :, :].rearrange("p a b -> p (a b)"), op=ALU.mult),
                                reads=[pk, "tri4"], writes=["cbm"])
                        dsl = slice(dd * 32, dd * 32 + 32)
                        S.op('dve', lambda v: v.tensor_tensor(out=dtt[:, 0, :], in0=dtr[:, dsl], in1=dtb[:, dsl], op=ALU.add),
                             reads=["dtr", "dtb"], writes=["dtt"])
                        S.op('act', lambda a: a.activation(out=dtt[:, 0, :], in_=dtt[:, 0, :], func=AF.Exp), reads=["dtt"], writes=["dtt"])
                        S.op('dve', lambda v: v.tensor_scalar(out=dtt[:, 0, :], in0=dtt[:, 0, :], scalar1=1.0, scalar2=None, op0=ALU.add),
                             reads=["dtt"], writes=["dtt"])
                        S.op('act', lambda a: a.activation(out=dtt[:, 0, :], in_=dtt[:, 0, :], func=AF.Ln), reads=["dtt"], writes=["dtt"])
                        S.op('dve', lambda v: v.tensor_tensor(out=dtt[:, 1, :], in0=dtt[:, 0, :], in1=Abc[:, dsl], op=ALU.mult),
                             reads=["dtt", "Abc"], writes=["dtt"])
                        psa, pak = next_ps()
                        S.op('pe', lambda t: t.matmul(psa[:, 0:32], lhsT=Tri, rhs=dtt[:, 1, :], start=True, stop=True),
                             reads=["sm", "dtt"], writes=[pak])
                        S.op('pe', lambda t: t.matmul(psa[:, 32:64], lhsT=ones_f, rhs=dtt[:, 1, :], start=True, stop=True),
                             reads=["cst", "dtt"], writes=[pak])
                        S.op('act', lambda a: a.copy(out=dtt[:, 2, :], in_=psa[:, 0:32]), reads=[pak], writes=["dtt"])
                        S.op('act', lambda a: a.activation(out=dtt[:, 3, :], in_=psa[:, 0:32], func=AF.Exp), reads=[pak], writes=["dtt"])
                        S.op('act', lambda a: a.activation(out=dtt[:, 4, :], in_=psa[:, 32:64], func=AF.Exp), reads=[pak], writes=["dtt"])
                        S.op('dve', lambda v: v.tensor_tensor(out=dtt[:, 6, :], in0=psa[:, 32:64], in1=dtt[:, 2, :], op=ALU.subtract),
                             reads=[pak, "dtt"], writes=["dtt"])
                        S.op('act', lambda a: a.activation(out=dtt[:, 6, :], in_=dtt[:, 6, :], func=AF.Exp), reads=["dtt"], writes=["dtt"])
                        S.op('dve', lambda v: v.tensor_tensor(out=dtt[:, 5, :], in0=dtt[:, 6, :], in1=dtt[:, 0, :], op=ALU.mult),
                             reads=["dtt"], writes=["dtt"])
                        S.op('dve', lambda v: v.tensor_tensor(out=xdt[:, :].rearrange("p (h q) -> p h q", q=64), in0=xs3,
                                                              in1=bc64(dtt[:, 0, :]), op=ALU.mult), reads=["x0", "dtt"], writes=["xdt"])
                        S.op('pool', lambda e: e.tensor_tensor(out=xw[:, :].rearrange("p (h q) -> p h q", q=64), in0=xs3,
                                                               in1=bc64(dtt[:, 5, :]), op=ALU.mult), reads=["x0", "dtt"], writes=["xw"])
                        for h in range(32):
                            if h % 3 == 0:
                                S.op('act', lambda a, h=h: a.activation(out=Rt[:, h, :], in_=Tri, func=AF.Copy, scale=dtt[:, 1, h:h + 1]),
                                     reads=["sm", "dtt"], writes=["Rt"])
                            elif h % 3 == 1:
                                S.op('dve', lambda v, h=h: v.tensor_scalar(out=Rt[:, h, :], in0=Tri, scalar1=dtt[:, 1, h:h + 1], scalar2=None,
                                                                           op0=ALU.mult), reads=["sm", "dtt"], writes=["Rt"])
                            else:
                                S.op('pool', lambda e, h=h: e.tensor_scalar(out=Rt[:, h, :], in0=Tri, scalar1=dtt[:, 1, h:h + 1], scalar2=None,
                                                                            op0=ALU.mult), reads=["sm", "dtt"], writes=["Rt"])
                        for g in range(4):
                            psY, pyk = next_ps()
                            for half in range(2):
                                hh = g * 8 + half * 4
                                sl = half
                                psd, pdk = next_ps()
                                S.op('pe', lambda t, psd=psd, hh=hh: t.matmul(
                                    psd[:, 0:512], lhsT=Mk, rhs=Rt[:, hh:hh + 4, :].rearrange("p a b -> p (a b)"), start=True, stop=True),
                                    reads=["sm", "Rt"], writes=[pdk])
                                S.op('act', lambda a, psd=psd, sl=sl: a.activation(out=Et[:, sl, :], in_=psd[:, 0:512], func=AF.Exp),
                                     reads=[pdk], writes=[f"Et{sl}"])
                                S.op('dve', lambda v, sl=sl, g=g: v.tensor_tensor(out=sct[:, sl, :], in0=Et[:, sl, :], in1=cbm[:, g, :], op=ALU.mult),
                                     reads=[f"Et{sl}", "cbm"], writes=[f"sct{sl}"])
                                for i in range(4):
                                    h = hh + i
                                    S.op('pe', lambda t, psY=psY, sl=sl, i=i, h=h, half=half: t.matmul(
                                        psY[:, (half * 4 + i) * 64:(half * 4 + i + 1) * 64], lhsT=sct[:, sl, i * 128:(i + 1) * 128],
                                        rhs=xdt[:, h * 64:(h + 1) * 64], start=True, stop=True),
                                        reads=[f"sct{sl}", "xdt"], writes=[pyk], signal=(i == 3))
                            psI, pik = next_ps()
                            S.op('pe', lambda t, psI=psI, g=g: t.matmul(psI[:, 0:512], lhsT=BCT[:, 4 + g, :], rhs=STb[:, g * 512:(g + 1) * 512],
                                                                        start=True, stop=True), reads=["BCT", "STb"], writes=[pik])
                            gs = slice(g * 512, (g + 1) * 512)
                            ts = g % 2
                            S.op('dve', lambda v, psI=psI, g=g, ts=ts: v.tensor_tensor(
                                out=tmpy[:, ts, :].rearrange("p (h q) -> p h q", q=64), in0=psI[:, 0:512].rearrange("p (h q) -> p h q", q=64),
                                in1=dtt[:, 3, g * 8:(g + 1) * 8].unsqueeze(2).to_broadcast([128, 8, 64]), op=ALU.mult),
                                reads=[pik, "dtt"], writes=[f"tmpy{ts}"])
                            if dd == 0:
                                S.op('dve', lambda v, psY=psY, ts=ts, gs=gs: v.tensor_tensor(out=yacc[:, gs], in0=tmpy[:, ts, :], in1=psY[:, 0:512], op=ALU.add),
                                     reads=[pyk, f"tmpy{ts}"], writes=["yacc"])
                            else:
                                S.op('pool', lambda e, ts=ts, gs=gs: e.tensor_tensor(out=yacc[:, gs], in0=yacc[:, gs], in1=tmpy[:, ts, :], op=ALU.add),
                                     reads=[f"tmpy{ts}", "yacc"], writes=["yacc"])
                                S.op('dve', lambda v, psY=psY, gs=gs: v.tensor_tensor(out=yacc[:, gs], in0=yacc[:, gs], in1=psY[:, 0:512], op=ALU.add),
                                     reads=[pyk, "yacc"], writes=["yacc"])
                            psS, psk = next_ps()
                            S.op('pe', lambda t, psS=psS, g=g, gs=gs: t.matmul(psS[:, 0:512], lhsT=xbf[:, 2048 + g * 128:2048 + (g + 1) * 128],
                                                                               rhs=xw[:, gs], start=True, stop=True), reads=["xbf", "xw"], writes=[psk])
                            S.op('pool', lambda e, g=g, gs=gs: e.tensor_tensor(
                                out=ST[:, gs].rearrange("p (h q) -> p h q", q=64), in0=ST[:, gs].rearrange("p (h q) -> p h q", q=64),
                                in1=dtt[:, 4, g * 8:(g + 1) * 8].unsqueeze(2).to_broadcast([128, 8, 64]), op=ALU.mult),
                                reads=["ST", "dtt"], writes=["ST"])
                            S.op('dve', lambda v, psS=psS, gs=gs: v.tensor_tensor(out=ST[:, gs], in0=ST[:, gs], in1=psS[:, 0:512], op=ALU.add),
                                 reads=[psk, "ST"], writes=["ST"])
                            S.op('act', lambda a, gs=gs: a.copy(out=STb[:, gs], in_=ST[:, gs]), reads=["ST"], writes=["STb"])
                        if dd == 0:
                            S.op('pool', lambda e: e.tensor_tensor(out=zt[:, :].rearrange("p (h q) -> p h q", q=64), in0=xs3,
                                                                   in1=bc64(dsk[:, :]), op=ALU.mult), reads=["x0", "dsk"], writes=["zt"])
                            S.op('dve', lambda v: v.tensor_tensor(out=yacc[:], in0=yacc[:], in1=zt[:], op=ALU.add), reads=["yacc", "zt"], writes=["yacc"])
                            S.dma('sp', y1_d[t0:t0 + 128, :], yacc[:], reads=["yacc"], writes=["y1_d"])
                        else:
                            S.op('act', lambda a: a.activation(out=zt[:], in_=zt[:], func=AF.Silu), reads=["zt"], writes=["zt"])
                            S.op('dve', lambda v: v.tensor_tensor(out=yacc[:], in0=yacc[:], in1=zt[:], op=ALU.mult), reads=["yacc", "zt"], writes=["yacc"])
                            S.op('dve', lambda v: v.scalar_tensor_tensor(out=ynb[:], in0=yacc[:], scalar=1.0, in1=yacc[:], op0=ALU.mult, op1=ALU.mult,
                                                                         accum_out=ssq[:, 0:1]), reads=["yacc"], writes=["ynb", "ssq"])
                            S.op('act', lambda a: a.activation(out=ssq[:, 1:2], in_=ssq[:, 0:1], func=AF.Sqrt, bias=epst[:, 0:1], scale=1.0 / 2048),
                                 reads=["ssq", "epst"], writes=["ssq"])
                            S.op('dve', lambda v: v.reciprocal(out=ssq[:, 2:3], in_=ssq[:, 1:2]), reads=["ssq"], writes=["ssq"])
                            S.op('dve', lambda v: v.scalar_tensor_tensor(out=ynb[:], in0=yacc[:], scalar=ssq[:, 2:3], in1=gn[:], op0=ALU.mult, op1=ALU.mult),
                                 reads=["yacc", "ssq", "gn"], writes=["ynb"])
                            for k0 in range(0, 16, 8):
                                pt, ptk = next_pst()
                                for jx in range(8):
                                    S.op('pe', lambda t, pt=pt, jx=jx, k0=k0: t.transpose(out=pt[:, jx * 128:(jx + 1) * 128],
                                                                                        in_=ynb[:, (k0 + jx) * 128:(k0 + jx + 1) * 128], identity=ident[:]),
                                         reads=["ynb", "ident"], writes=[ptk], signal=(jx == 7))
                                S.op('act', lambda a, pt=pt, k0=k0: a.copy(out=sTt[:, k0:k0 + 8, :], in_=pt[:, :].rearrange("p (j t) -> p j t", j=8)),
                                     reads=[ptk], writes=["sTt"])
                            S.dma('sp', s_d[:, :, t0:t0 + 128].rearrange("c p t -> p c t"), sTt[:], reads=["sTt"], writes=["s_d"])

            if "p3" in stages:
              with ExitStack() as es4:
                S.barrier()
                def sb4(name, shape, dt):
                    return es4.enter_context(nc.sbuf_tensor(f"s3_{name}_{layer}", shape, dt))
                xt = sb4("xt", [128, 4, D], F32)
                hT = sb4("hT", [128, 32, TT], BF16)
                hid = sb4("hid", [128, 20, TT], BF16)
                hb = sb4("hb", [128, D], BF16)
                gbc = sb4("gbc", [128, D], F32)
                ss = sb4("ss", [128, 8], F32)
                stg = sb4("stg", [128, 4, 512], F32)
                gat = sb4("gat", [128, 2, 2, TT], BF16)
                stg_rr = [0]

                def next_stg():
                    i = stg_rr[0] % 4
                    stg_rr[0] += 1
                    return i
                norm_to_hT, ffn = make_fns(xt, hT, hid, hb, gbc, ss, stg, next_stg, layer)
                aT = hid[:, 16:20, :]
                sT3 = hid[:, 0:16, :]
                wao = w["w_attn_out"][layer]
                wso = w["w_ssm_out"][layer]
                cnt = 0
                for tile in range(NTILE):
                    t0 = tile * TT
                    S.dma('sp', xt[:], y[t0:t0 + TT, :].rearrange("(b p) f -> p b f", p=128), reads=["ydram"], writes=["xt"])
                    S.dma('sp', aT, o_d[:, :, t0:t0 + TT].rearrange("c p t -> p c t"), reads=["o_d"], writes=["hid"])
                    S.dma('sp', sT3, s_d[:, :, t0:t0 + TT].rearrange("c p t -> p c t"), reads=["s_d"], writes=["hid"])
                    for b0 in range(0, D, 256):
                        wat, wak, _ = wload(wao, b0, 256)
                        wst, wsk, _ = wload(wso, b0, 256)
                        for ci in range(2):
                            fc = b0 // 128 + ci
                            gsl = cnt % 2
                            cnt += 1
                            S.dma('sp', gat[:, gsl, 0, :], g_d[fc, :, t0:t0 + TT], reads=["g_d"], writes=[f"gat{gsl}"])
                            S.dma('sp', gat[:, gsl, 1, :], g_d[32 + fc, :, t0:t0 + TT], reads=["g_d"], writes=[f"gat{gsl}"])
                            psa, pak = next_ps()
                            mm_group(psa[:, 0:TT], pak, 4, lambda kc, ci=ci, wat=wat: wat[:, kc, ci * 128:(ci + 1) * 128],
                                     lambda kc: aT[:, kc, :], [wak, "hid"])
                            pss, psk = next_ps()
                            mm_group(pss[:, 0:TT], psk, 16, lambda kc, ci=ci, wst=wst: wst[:, kc, ci * 128:(ci + 1) * 128],
                                     lambda kc: sT3[:, kc, :], [wsk, "hid"])
                            si = next_stg()
                            sj = next_stg()
                            S.op('dve', lambda v, psa=psa, si=si, gsl=gsl: v.tensor_tensor(out=stg[:, si, :], in0=psa[:, 0:TT], in1=gat[:, gsl, 0, :], op=ALU.mult),
                                 reads=[pak, f"gat{gsl}"], writes=[f"stg{si}"])
                            S.op('dve', lambda v, pss=pss, sj=sj, gsl=gsl: v.tensor_tensor(out=stg[:, sj, :], in0=pss[:, 0:TT], in1=gat[:, gsl, 1, :], op=ALU.mult),
                                 reads=[psk, f"gat{gsl}"], writes=[f"stg{sj}"])
                            S.op('dve', lambda e, si=si, sj=sj, fc=fc: e.tensor_tensor(out=hT[:, fc, :], in0=stg[:, si, :], in1=stg[:, sj, :], op=ALU.add),
                                 reads=[f"stg{si}", f"stg{sj}"], writes=["hT"])

                    def ev_out(ps, pskey, b, c0, wd):
                        S.op('dve', lambda v: v.tensor_tensor(out=xt[:, b, c0:c0 + wd], in0=xt[:, b, c0:c0 + wd], in1=ps[:, 0:wd], op=ALU.add),
                             reads=[pskey, "xt"], writes=["xt"])
                    gemm_tm(w["w_out"][layer], 0, D, hT, "hT", ev_out)
                    norm_to_hT("ffn2_norm")
                    ffn(w["ffn2_w_gate"][layer], w["ffn2_w_up"][layer], w["ffn2_w_down"][layer])
                    S.dma('sp', y[t0:t0 + TT, :].rearrange("(b p) f -> p b f", p=128), xt[:], reads=["xt"], writes=["ydram"])

        S.finish(["ydram"])
    return nc


def host_consts(conn):
    c = np.zeros((128, NCST), np.float32)
    for m in range(128):
        if m < 64:
            c[m + 64, C_RM + m] = -1.0
        else:
            c[m - 64, C_RM + m] = 1.0
    c[:, C_ONES:C_ONES + 128] = 1.0
    k = np.arange(128)[:, None]
    s = np.arange(128)[None, :]
    c[:, C_MF:C_MF + 128] = (k > s)
    c[:, C_TF:C_TF + 128] = (k <= s)
    c[:, C_MB:C_MB + 128] = (k < s)
    c[:, C_TB:C_TB + 128] = (k >= s)
    j = np.arange(256)[None, :]
    band = (np.abs(k + 64 - j) <= 64).astype(np.float32)
    lo = band.copy()
    lo[:, 0:64] *= conn
    hi = band.copy()
    hi[:, 192:256] *= conn
    c[:, C_AM:C_AM + 256] = band
    c[:, C_AM + 256:C_AM + 512] = lo
    c[:, C_AM + 512:C_AM + 768] = hi
    c[:, C_ID:C_ID + 128] = np.eye(128, dtype=np.float32)
    c[:, C_CV] = conn
    c[:, C_CV + 1] = 1.0
    c[127, C_CV + 1] = conn
    return c


def host_rope(seqlen):
    pos = (np.arange(T) % seqlen).astype(np.float32)
    inv = (10000.0 ** (-np.arange(0, 128, 2, dtype=np.float32) / 128)).astype(np.float32)
    ang = pos[None, :] * np.concatenate([inv, inv])[:, None]
    return np.cos(ang).astype(np.float32), np.sin(ang).astype(np.float32)


def tile_all(inputs, n_layers=L_FULL):
    tiled = {}
    for name in BIG_SEGS:
        W = np.asarray(inputs[name], dtype=np.float32)[:n_layers]
        for si, (st, en, bw) in enumerate(BIG_SEGS[name]):
            tiled[(name, si)] = tile_weight(W, st, en, bw)
    return tiled


def make_in_map(xc, inputs, conn, n_layers=L_FULL, tiled=None):
    if tiled is None:
        tiled = tile_all(inputs, n_layers)
    m = {"x": np.ascontiguousarray(xc, dtype=np.float32)}
    for name, shp in W_SPECS:
        if name in BIG_SEGS:
            for si, (st, en, bw) in enumerate(BIG_SEGS[name]):
                m[f"{name}_t{si}"] = tiled[(name, si)]
        else:
            m[name] = np.asarray(inputs[name], dtype=np.float32)[:n_layers]
    m["cst"] = host_consts(conn)
    cs, sn = host_rope(4096 if conn == 1.0 else 2048)
    m["cosT"], m["sinT"] = cs, sn
    return m


def kernel(**inputs):
    xp = np.asarray(inputs["x_prompt"], dtype=np.float32)
    xs = np.asarray(inputs["x_sample"], dtype=np.float32)
    nc = build_nc()
    tiled = tile_all(inputs)
    in_maps = []
    for c in range(8):
        if c < 4:
            in_maps.append(make_in_map(xp[c], inputs, 1.0, tiled=tiled))
        else:
            in_maps.append(make_in_map(xs[2 * (c - 4):2 * (c - 4) + 2].reshape(T, D), inputs, 0.0, tiled=tiled))
    res = run_bass_kernel_spmd(nc, in_maps, core_ids=list(range(8)))
    outs = [r["y"] for r in res.results]
    y_prompt = np.stack(outs[:4], axis=0)
    y_sample = np.concatenate(outs[4:], axis=0).reshape(8, 2048, D)
    return (y_prompt, y_sample)
```
